# Optimizing a Trainium2 kernel written in Bass

```python
import math
import jax, jax.numpy as jnp
from jax import lax
import numpy as np

D_MODEL = 2048
BATCH = 4
SEQ = 2048
DEPTH = 1
DEC_BATCH = 128
DEC_SEQ = 8
PAST_LEN = 16384
PAGE_SIZE = 128

SSD_WIDTH = D_MODEL
SSD_HEAD_DIM = 64
SSD_HEADS = SSD_WIDTH // SSD_HEAD_DIM
SSD_GROUPS = 4
SSD_HEADS_PER_GROUP = SSD_HEADS // SSD_GROUPS
SSD_STATE = 128
SSD_CONV = 4
SSD_CONV_DIM = SSD_WIDTH + 2 * SSD_GROUPS * SSD_STATE
SSD_CHUNK = 128
SC_WIDTH = D_MODEL
SC_CONV = 3
MIX_WIDTH = SSD_WIDTH + SC_WIDTH
IN_DIM = SSD_WIDTH + SSD_CONV_DIM + SSD_HEADS + 3 * SC_WIDTH
MEM_TOKENS = 256
MEM_HEADS = 4
MEM_HEAD_DIM = D_MODEL // MEM_HEADS
PEER_HEADS = 8
PEER_NKEYS = 128
PEER_EXPERTS = PEER_NKEYS * PEER_NKEYS
PEER_QDIM = 256
PEER_HALF = PEER_QDIM // 2
PEER_TOPK = 16
PEER_BLOCK = 64
EPS = 1e-6

kernel_name = 'hymba_ssd_shortconv_peer_memxattn_step'


def rmsnorm(x, g):
    xf = x.astype(jnp.float32)
    y = xf * lax.rsqrt(jnp.mean(xf * xf, axis=-1, keepdims=True) + EPS)
    return (y * g.astype(jnp.float32)).astype(x.dtype)


def causal_dwconv(x, buf, w, b=None):
    k = w.shape[0]
    n = x.shape[1]
    xp = jnp.concatenate([buf.astype(x.dtype), x], axis=1)
    y = xp[:, 0:n] * w[0]
    for j in range(1, k):
        y = y + xp[:, j:j + n] * w[j]
    if b is not None:
        y = y + b
    return y, xp[:, n:]


def ssd_scan(x, dt, a_log, bm, cm, d_skip, init_state):
    f32 = jnp.float32
    G, R, P, N = SSD_GROUPS, SSD_HEADS_PER_GROUP, SSD_HEAD_DIM, SSD_STATE
    b, L = x.shape[0], x.shape[1]
    cs = min(SSD_CHUNK, L)
    nc = -(-L // cs)
    pad = nc * cs - L
    xf, dtf, bf, cf = x.astype(f32), dt.astype(f32), bm.astype(f32), cm.astype(f32)
    if pad:
        padt = lambda t: jnp.pad(t, [(0, 0), (0, pad)] + [(0, 0)] * (t.ndim - 2))
        xf, dtf, bf, cf = padt(xf), padt(dtf), padt(bf), padt(cf)
    A = -jnp.exp(a_log.astype(f32)).reshape(G, R)
    a = (dtf * A).reshape(b, nc, cs, G, R)
    xdt = (xf * dtf[..., None]).reshape(b, nc, cs, G, R, P)
    bc = bf.reshape(b, nc, cs, G, N)
    cc = cf.reshape(b, nc, cs, G, N)
    a_cs = jnp.cumsum(jnp.transpose(a, (0, 3, 4, 1, 2)), axis=-1)
    tril = jnp.tril(jnp.ones((cs, cs), dtype=bool))
    lmat = jnp.exp(jnp.where(tril, a_cs[..., :, None] - a_cs[..., None, :], -jnp.inf))
    cb = jnp.einsum('bclgn,bcsgn->bgcls', cc, bc)
    y_diag = jnp.einsum('bgcls,bgrcls,bcsgrp->bclgrp', cb, lmat, xdt)
    decay_states = jnp.exp(a_cs[..., -1:] - a_cs)
    states = jnp.einsum('bcsgn,bgrcs,bcsgrp->bcgrpn', bc, decay_states, xdt)
    states = jnp.concatenate([init_state.astype(f32)[:, None], states], axis=1)
    tot = jnp.cumsum(jnp.pad(a_cs[..., -1], [(0, 0), (0, 0), (0, 0), (1, 0)]), axis=-1)
    tril2 = jnp.tril(jnp.ones((nc + 1, nc + 1), dtype=bool))
    decay_chunk = jnp.exp(jnp.where(tril2, tot[..., :, None] - tot[..., None, :], -jnp.inf))
    new_states = jnp.einsum('bgrzc,bcgrpn->bzgrpn', decay_chunk, states)
    y_off = jnp.einsum('bclgn,bcgrpn,bgrcl->bclgrp', cc, new_states[:, :-1], jnp.exp(a_cs))
    y = (y_diag + y_off).reshape(b, nc * cs, G, R, P)[:, :L]
    y = y + d_skip.astype(f32).reshape(G, R)[:, :, None] * x.astype(f32)
    return y, new_states[:, -1]


def mixer(xn, ssm_state, ssd_buf, sc_buf, w_in, conv_w, conv_b, dt_bias, a_log, d_skip, ssd_norm, sc_w, w_out):
    f32 = jnp.float32
    b, L, _ = xn.shape
    G, R, P, N = SSD_GROUPS, SSD_HEADS_PER_GROUP, SSD_HEAD_DIM, SSD_STATE
    s1 = SSD_WIDTH
    s2 = s1 + SSD_CONV_DIM
    s3 = s2 + SSD_HEADS
    s4 = s3 + SC_WIDTH
    s5 = s4 + SC_WIDTH
    h = xn @ w_in
    z, xbc, dt, sc_h, sc_b, sc_c = jnp.split(h, [s1, s2, s3, s4, s5], axis=-1)
    xbc, new_ssd_buf = causal_dwconv(xbc, ssd_buf, conv_w, conv_b)
    xbc = jax.nn.silu(xbc)
    xs, bm, cm = jnp.split(xbc, [SSD_WIDTH, SSD_WIDTH + G * N], axis=-1)
    dt = jax.nn.softplus(dt.astype(f32) + dt_bias.astype(f32))
    y, new_state = ssd_scan(xs.reshape(b, L, G, R, P), dt.reshape(b, L, G, R), a_log,
                            bm.reshape(b, L, G, N), cm.reshape(b, L, G, N), d_skip,
                            ssm_state.reshape(b, G, R, P, N))
    yg = (y.reshape(b, L, SSD_WIDTH) * jax.nn.silu(z.astype(f32))).reshape(b, L, G, SSD_WIDTH // G)
    yg = yg * lax.rsqrt(jnp.mean(yg * yg, axis=-1, keepdims=True) + EPS)
    y_ssd = (yg.reshape(b, L, SSD_WIDTH) * ssd_norm.astype(f32)).astype(xn.dtype)
    cv, new_sc_buf = causal_dwconv(sc_c * sc_h, sc_buf, sc_w)
    y_sc = sc_b * cv
    out = jnp.concatenate([y_ssd, y_sc], axis=-1) @ w_out
    return out, new_state.reshape(b, SSD_HEADS, P, N), new_ssd_buf, new_sc_buf


def mem_kv(mem, g, wk, wv):
    b = mem.shape[0]
    mn = rmsnorm(mem, g)
    k = (mn @ wk).reshape(b, MEM_TOKENS, MEM_HEADS, MEM_HEAD_DIM)
    v = (mn @ wv).reshape(b, MEM_TOKENS, MEM_HEADS, MEM_HEAD_DIM)
    return k, v


def mem_attend(xn, k, v, wq, wo):
    b, L, _ = xn.shape
    q = (xn @ wq).reshape(b, L, MEM_HEADS, MEM_HEAD_DIM)
    s = jnp.einsum('blhd,bmhd->bhlm', q, k.astype(q.dtype)).astype(jnp.float32) * (1.0 / math.sqrt(MEM_HEAD_DIM))
    p = jax.nn.softmax(s, axis=-1)
    o = jnp.einsum('bhlm,bmhd->blhd', p.astype(v.dtype), v).reshape(b, L, D_MODEL)
    return (o @ wo).astype(xn.dtype)


def peer(xn, wq, sub_keys, u, v):
    f32 = jnp.float32
    b, L, D = xn.shape
    T = b * L
    K = PEER_TOPK
    xt = xn.reshape(T, D)
    q = (xt @ wq).reshape(T, PEER_HEADS, 2, PEER_HALF)
    s = jnp.einsum('thid,hikd->thik', q, sub_keys).astype(f32)
    s1, i1 = lax.top_k(s[:, :, 0], K)
    s2, i2 = lax.top_k(s[:, :, 1], K)
    comb = (s1[..., :, None] + s2[..., None, :]).reshape(T, PEER_HEADS, K * K)
    cid = (i1[..., :, None] * PEER_NKEYS + i2[..., None, :]).reshape(T, PEER_HEADS, K * K)
    sf, pos = lax.top_k(comb, K)
    eid = jnp.take_along_axis(cid, pos, axis=-1).reshape(T, PEER_HEADS * K)
    g = jax.nn.softmax(sf, axis=-1).reshape(T, PEER_HEADS * K)
    nb = -(-T // PEER_BLOCK)
    pad = nb * PEER_BLOCK - T
    xb = jnp.pad(xt, ((0, pad), (0, 0))).reshape(nb, PEER_BLOCK, D)
    eb = jnp.pad(eid, ((0, pad), (0, 0))).reshape(nb, PEER_BLOCK, PEER_HEADS * K)
    gb = jnp.pad(g, ((0, pad), (0, 0))).reshape(nb, PEER_BLOCK, PEER_HEADS * K)

    def block(args):
        xi, ei, gi = args
        act = jax.nn.gelu(jnp.einsum('td,tkd->tk', xi, u[ei]).astype(f32), approximate=False)
        return jnp.einsum('tk,tkd->td', (gi * act).astype(v.dtype), v[ei]).astype(xi.dtype)

    out = lax.map(block, (xb, eb, gb)).reshape(nb * PEER_BLOCK, D)[:T]
    return out.reshape(b, L, D)


def layer(x, mem_k, mem_v, ssm_state, ssd_buf, sc_buf, mix_p, mem_p, ffn_p):
    norm_mix, w_in, conv_w, conv_b, dt_bias, a_log, d_skip, ssd_norm, sc_w, w_out = mix_p
    norm_mem_q, w_mem_q, w_mem_o = mem_p
    norm_ffn, w_peer_q, sub_keys, peer_u, peer_v = ffn_p
    m, new_ssm, new_ssd_buf, new_sc_buf = mixer(rmsnorm(x, norm_mix), ssm_state, ssd_buf, sc_buf, w_in, conv_w,
                                                conv_b, dt_bias, a_log, d_skip, ssd_norm, sc_w, w_out)
    x = x + m
    x = x + mem_attend(rmsnorm(x, norm_mem_q), mem_k, mem_v, w_mem_q, w_mem_o)
    x = x + peer(rmsnorm(x, norm_ffn), w_peer_q, sub_keys, peer_u, peer_v)
    return x, new_ssm, new_ssd_buf, new_sc_buf


def setup_inputs(seed: int = 0) -> dict:
    key = jax.random.key(seed)
    ks = jax.random.split(key, 32)
    f32 = jnp.float32
    nrm = lambda k, shape, s: jax.random.normal(k, shape, f32) * s
    gain = lambda k, shape: 1.0 + 0.02 * jax.random.normal(k, shape, f32)
    dt0 = jnp.exp(jax.random.uniform(ks[13], (DEPTH, SSD_HEADS), f32, math.log(1e-3), math.log(1e-1)))
    return {
        'x_prompt': nrm(ks[0], (BATCH, SEQ, D_MODEL), 1.0),
        'x_sample': nrm(ks[1], (DEC_BATCH, DEC_SEQ, D_MODEL), 1.0),
        'mem_prompt': nrm(ks[2], (BATCH, MEM_TOKENS, D_MODEL), 1.0),
        'state_ssm': nrm(ks[3], (DEPTH, DEC_BATCH, SSD_HEADS, SSD_HEAD_DIM, SSD_STATE), 0.5),
        'state_ssd_conv': nrm(ks[4], (DEPTH, DEC_BATCH, SSD_CONV - 1, SSD_CONV_DIM), 1.0),
        'state_short_conv': nrm(ks[5], (DEPTH, DEC_BATCH, SC_CONV - 1, SC_WIDTH), 1.0),
        'cache_mem_k': nrm(ks[6], (DEPTH, DEC_BATCH, MEM_TOKENS, MEM_HEADS, MEM_HEAD_DIM), 1.0),
        'cache_mem_v': nrm(ks[7], (DEPTH, DEC_BATCH, MEM_TOKENS, MEM_HEADS, MEM_HEAD_DIM), 1.0),
        'norm_mix': gain(ks[8], (DEPTH, D_MODEL)),
        'w_in': nrm(ks[9], (DEPTH, D_MODEL, IN_DIM), D_MODEL ** -0.5),
        'ssd_conv_w': nrm(ks[10], (DEPTH, SSD_CONV, SSD_CONV_DIM), SSD_CONV ** -0.5),
        'ssd_conv_b': nrm(ks[11], (DEPTH, SSD_CONV_DIM), 0.02),
        'ssd_dt_bias': dt0 + jnp.log(-jnp.expm1(-dt0)),
        'ssd_a_log': jnp.log(jax.random.uniform(ks[12], (DEPTH, SSD_HEADS), f32, 1.0, 16.0)),
        'ssd_d': 1.0 + nrm(ks[14], (DEPTH, SSD_HEADS), 0.1),
        'ssd_norm': gain(ks[15], (DEPTH, SSD_WIDTH)),
        'sc_conv_w': nrm(ks[16], (DEPTH, SC_CONV, SC_WIDTH), SC_CONV ** -0.5),
        'w_out': nrm(ks[17], (DEPTH, MIX_WIDTH, D_MODEL), MIX_WIDTH ** -0.5),
        'norm_mem_q': gain(ks[18], (DEPTH, D_MODEL)),
        'norm_mem_kv': gain(ks[19], (DEPTH, D_MODEL)),
        'w_mem_q': nrm(ks[20], (DEPTH, D_MODEL, D_MODEL), D_MODEL ** -0.5),
        'w_mem_k': nrm(ks[21], (DEPTH, D_MODEL, D_MODEL), D_MODEL ** -0.5),
        'w_mem_v': nrm(ks[22], (DEPTH, D_MODEL, D_MODEL), D_MODEL ** -0.5),
        'w_mem_o': nrm(ks[23], (DEPTH, D_MODEL, D_MODEL), D_MODEL ** -0.5),
        'norm_ffn': gain(ks[24], (DEPTH, D_MODEL)),
        'w_peer_q': nrm(ks[25], (DEPTH, D_MODEL, PEER_HEADS * PEER_QDIM), D_MODEL ** -0.5),
        'peer_sub_keys': nrm(ks[26], (DEPTH, PEER_HEADS, 2, PEER_NKEYS, PEER_HALF), PEER_HALF ** -0.5),
        'peer_u': nrm(ks[27], (DEPTH, PEER_EXPERTS, D_MODEL), D_MODEL ** -0.5),
        'peer_v': nrm(ks[28], (DEPTH, PEER_EXPERTS, D_MODEL), 0.1),
        'norm_final': gain(ks[29], (D_MODEL,)),
    }


def reference(x_prompt, x_sample, mem_prompt, state_ssm, state_ssd_conv, state_short_conv, cache_mem_k,
              cache_mem_v, norm_mix, w_in, ssd_conv_w, ssd_conv_b, ssd_dt_bias, ssd_a_log, ssd_d, ssd_norm,
              sc_conv_w, w_out, norm_mem_q, norm_mem_kv, w_mem_q, w_mem_k, w_mem_v, w_mem_o, norm_ffn,
              w_peer_q, peer_sub_keys, peer_u, peer_v, norm_final):
    bp = x_prompt.shape[0]
    hp, hs = x_prompt, x_sample
    p_ssm, p_ssd_conv, p_sc, p_mk, p_mv = [], [], [], [], []
    s_ssm, s_ssd_conv, s_sc = [], [], []
    for i in range(DEPTH):
        mix_p = (norm_mix[i], w_in[i], ssd_conv_w[i], ssd_conv_b[i], ssd_dt_bias[i], ssd_a_log[i], ssd_d[i],
                 ssd_norm[i], sc_conv_w[i], w_out[i])
        mem_p = (norm_mem_q[i], w_mem_q[i], w_mem_o[i])
        ffn_p = (norm_ffn[i], w_peer_q[i], peer_sub_keys[i], peer_u[i], peer_v[i])
        mk, mv = mem_kv(mem_prompt, norm_mem_kv[i], w_mem_k[i], w_mem_v[i])
        z_ssm = jnp.zeros((bp, SSD_HEADS, SSD_HEAD_DIM, SSD_STATE), jnp.float32)
        z_ssd_buf = jnp.zeros((bp, SSD_CONV - 1, SSD_CONV_DIM), x_prompt.dtype)
        z_sc_buf = jnp.zeros((bp, SC_CONV - 1, SC_WIDTH), x_prompt.dtype)
        hp, n_ssm, n_ssd_buf, n_sc_buf = layer(hp, mk, mv, z_ssm, z_ssd_buf, z_sc_buf, mix_p, mem_p, ffn_p)
        p_ssm.append(n_ssm)
        p_ssd_conv.append(n_ssd_buf)
        p_sc.append(n_sc_buf)
        p_mk.append(mk)
        p_mv.append(mv)
        hs, m_ssm, m_ssd_buf, m_sc_buf = layer(hs, cache_mem_k[i], cache_mem_v[i], state_ssm[i],
                                               state_ssd_conv[i], state_short_conv[i], mix_p, mem_p, ffn_p)
        s_ssm.append(m_ssm)
        s_ssd_conv.append(m_ssd_buf)
        s_sc.append(m_sc_buf)
    y_prompt = rmsnorm(hp, norm_final)
    y_sample = rmsnorm(hs, norm_final)
    return (y_prompt, y_sample, jnp.stack(p_ssm), jnp.stack(p_ssd_conv), jnp.stack(p_sc), jnp.stack(p_mk),
            jnp.stack(p_mv), jnp.stack(s_ssm), jnp.stack(s_ssd_conv), jnp.stack(s_sc))
```

```python
import math
import numpy as np
from contextlib import ExitStack
import concourse.bass as bass
import concourse.mybir as mybir
from concourse.bass_utils import run_bass_kernel_spmd

F32 = mybir.dt.float32
BF16 = mybir.dt.bfloat16
I32 = mybir.dt.int32
U32 = mybir.dt.uint32
AF = mybir.ActivationFunctionType
ALU = mybir.AluOpType

NT, NPR, NS = 1152, 1024, 128
SEGW = 640
EPS = 1e-6
STAGE = 3
KSTOP = 99
KSUB = 99
KDBG = 0


class Ev:
    __slots__ = ("sem", "val")

    def __init__(self, sem, val):
        self.sem, self.val = sem, val


class Buf:
    def __init__(self, name):
        self.name = name
        self.lw = None
        self.rd = {}
        self.dsem = None
        self.dval = 0


class Tile:
    def __init__(self, t, b):
        self.t, self.b = t, b


class KB:
    def __init__(self):
        self.nc = bass.Bass("TRN2", target_bir_lowering=False)
        self.es = ExitStack()
        nc = self.nc
        self.eng = {"pe": nc.tensor, "dve": nc.vector, "act": nc.scalar, "pool": nc.gpsimd, "sp": nc.sync}
        self.esem = {k: self.es.enter_context(nc.semaphore("s_" + k)) for k in ("pe", "dve", "act", "pool")}
        self.ecnt = {k: 0 for k in self.esem}
        self.waited = {k: {} for k in self.eng}
        self.bufs = []
        self.dram = {}
        self.n = 0
        self.named = {}

    def din(self, name, shape, dt=F32):
        self.dram[name] = self.nc.dram_tensor(name, list(shape), dt, kind="ExternalInput").ap()
        return self.dram[name]

    def dout(self, name, shape, dt=F32):
        self.dram[name] = self.nc.dram_tensor(name, list(shape), dt, kind="ExternalOutput").ap()
        return self.dram[name]

    def buf(self, name):
        b = Buf(name)
        self.bufs.append(b)
        return b

    def sb(self, stack, name, shape, dt=F32):
        self.n += 1
        t = stack.enter_context(self.nc.sbuf_tensor(f"{name}_{self.n}", list(shape), dt))
        return Tile(t, self.buf(name))

    def psum(self, stack, name, shape, dt=F32):
        t = stack.enter_context(self.nc.psum_tensor(name, list(shape), dt))
        return Tile(t, self.buf(name))

    def _wait(self, e, ev):
        if ev is None:
            return
        if e == "pe" and ev.sem is self.esem["pe"]:
            return
        w = self.waited[e]
        k = id(ev.sem)
        if w.get(k, 0) >= ev.val:
            return
        self.eng[e].wait_ge(ev.sem, ev.val)
        w[k] = ev.val

    def _deps(self, e, R, W):
        for b in R:
            self._wait(e, b.lw)
        for b in W:
            self._wait(e, b.lw)
            for r in b.rd.values():
                self._wait(e, r)

    def _commit(self, ev, R, W):
        for b in R:
            k = id(ev.sem)
            o = b.rd.get(k)
            if o is None or o.val < ev.val:
                b.rd[k] = ev
        for b in W:
            b.lw = ev
            b.rd = {}

    def op(self, e, fn, R=(), W=()):
        self._deps(e, R, W)
        ins = fn(self.eng[e])
        self.ecnt[e] += 1
        ev = Ev(self.esem[e], self.ecnt[e])
        ins.then_inc(ev.sem, 1)
        self._commit(ev, R, W)
        return ev

    def mm(self, out, pairs, R=(), W=(), start=True, stop=True):
        self._deps("pe", R, W)
        n = len(pairs)
        ins = None
        for i, (l, r) in enumerate(pairs):
            ins = self.nc.tensor.matmul(out, lhsT=l, rhs=r, start=(start and i == 0), stop=(stop and i == n - 1))
        self.ecnt["pe"] += 1
        ev = Ev(self.esem["pe"], self.ecnt["pe"])
        ins.then_inc(ev.sem, 1)
        self._commit(ev, R, W)
        return ev

    def tr(self, items, ident, R=(), W=()):
        self._deps("pe", R, W)
        ins = None
        for o, i in items:
            ins = self.nc.tensor.transpose(out=o, in_=i, identity=ident)
        self.ecnt["pe"] += 1
        ev = Ev(self.esem["pe"], self.ecnt["pe"])
        ins.then_inc(ev.sem, 1)
        self._commit(ev, R, W)
        return ev

    def dma(self, pairs, R=(), W=(), q="sp", owner=None):
        self._deps(q, R, W)
        owner = owner or (W[0] if W else R[0])
        if owner.dsem is None:
            if owner.name in self.named:
                owner.dsem, owner.dval = self.named[owner.name]
            else:
                owner.dsem = self.es.enter_context(self.nc.semaphore("d%d_%s" % (len(self.bufs), owner.name)))
        for o, i in pairs:
            self.eng[q].dma_start(out=o, in_=i).then_inc(owner.dsem, 16)
            owner.dval += 16
        self.named[owner.name] = (owner.dsem, owner.dval)
        ev = Ev(owner.dsem, owner.dval)
        self._commit(ev, R, W)
        return ev

    def gather(self, out, table, idx_ap, R=(), W=()):
        q = "pool"
        self._deps(q, R, W)
        owner = W[0]
        if owner.dsem is None:
            owner.dsem = self.es.enter_context(self.nc.semaphore("g%d_%s" % (len(self.bufs), owner.name)))
        self.nc.gpsimd.indirect_dma_start(
            out=out, out_offset=None, in_=table, in_offset=bass.IndirectOffsetOnAxis(ap=idx_ap, axis=0)
        ).then_inc(owner.dsem, 16)
        owner.dval += 16
        ev = Ev(owner.dsem, owner.dval)
        self._commit(ev, R, W)
        return ev

    def barrier(self):
        evs = [Ev(self.esem[k], self.ecnt[k]) for k in self.esem if self.ecnt[k] > 0]
        evs += [Ev(b.dsem, b.dval) for b in self.bufs if b.dsem is not None and b.dval > 0]
        for e in self.eng:
            for ev in evs:
                if e in self.esem and ev.sem is self.esem[e]:
                    pass
                w = self.waited[e]
                k = id(ev.sem)
                if w.get(k, 0) >= ev.val:
                    continue
                self.eng[e].wait_ge(ev.sem, ev.val)
                w[k] = ev.val

    def finish(self):
        for b in self.bufs:
            if b.dsem is not None and b.dval > 0:
                self._wait("sp", Ev(b.dsem, b.dval))
        for k in self.esem:
            if self.ecnt[k] > 0:
                self._wait("sp", Ev(self.esem[k], self.ecnt[k]))


def bl(ap, k, n):
    return ap.unsqueeze(2).to_broadcast([128, k, n])


def bm(ap, k, n):
    return ap.unsqueeze(1).to_broadcast([128, k, n])


PV_GMIX, PV_GMQ, PV_GMKV, PV_GFFN, PV_GFIN, PV_SSDN = 0, 16, 32, 48, 64, 80
PV_CW, PV_CB, PV_SCW = 96, 192, 216
NPV = 264
C_ID, C_TRP, C_NGP, C_TRS, C_NGS, C_TOTS, C_RM, C_SEL = 0, 128, 256, 384, 512, 640, 768, 784
C_IOTA = 784 + 2048
C_IO128 = 784 + 2048 + 16
NCST = 784 + 2048 + 16 + 128


def build():
    kb = KB()
    nc = kb.nc
    G = ExitStack()
    xT = kb.din("xT", [2048, NT])
    xpT = kb.din("xpT", [2048, NPR])
    flag = kb.din("flag", [128, 1])
    pvec = kb.din("pvec", [128, NPV])
    tokp = kb.din("tokp", [128, 96])
    cst = kb.din("cst", [128, NCST])
    wz = kb.din("wz", [16, 128, 16, 128])
    wxbc = kb.din("wxbc", [24, 128, 16, 128])
    wdt = kb.din("wdt", [128, 16, 32])
    wsch = kb.din("wsch", [16, 128, 16, 128])
    wscb = kb.din("wscb", [16, 128, 16, 128])
    wscc = kb.din("wscc", [16, 128, 16, 128])
    wout = kb.din("wout", [8, 16, 128, 4, 128])
    stT = kb.din("stT", [16, 128, 2048])
    stconvT = kb.din("stconvT", [24, 128, 48])
    stscT = kb.din("stscT", [16, 128, 32])
    yT = kb.dout("yT", [2048, NT])
    o_pssm = kb.dout("o_pssm", [128, 2048])
    o_pconv = kb.dout("o_pconv", [128, 24 * 3])
    o_psc = kb.dout("o_psc", [128, 16 * 2])
    o_sssm = kb.dout("o_sssm", [16, 128, 2048])
    o_sconv = kb.dout("o_sconv", [128, 24 * 48])
    o_ssc = kb.dout("o_ssc", [128, 16 * 32])
    if KDBG:
        dbg1 = kb.dout("dbg1", [128, 512])
        dbg2 = kb.dout("dbg2", [128, 512])
    if STAGE >= 2:
        memT = kb.din("memT", [2048, 256])
        wmq = kb.din("wmq", [16, 128, 16, 128])
        wmk = kb.din("wmk", [16, 128, 16, 128])
        wmv = kb.din("wmv", [16, 128, 16, 128])
        wmo = kb.din("wmo", [16, 128, 16, 128])
        ckT = kb.din("ckT", [16, 2048, 256])
        cv = kb.din("cv", [16, 256, 2048])
        o_mk = kb.dout("o_mk", [2048, 256])
        o_mv = kb.dout("o_mv", [2048, 256])
    if STAGE >= 3:
        wpq = kb.din("wpq", [16, 128, 16, 128])
        skT = kb.din("skT", [16, 128, 128])
        pu = kb.din("pu", [16384, 2048])
        pv = kb.din("pv", [16384, 2048])
        Gd = nc.dram_tensor("Gd", [128, 128, SEGW], BF16, kind="Internal").ap()

    XT = kb.sb(G, "XT", [128, 16, SEGW])
    PV = kb.sb(G, "PV", [128, NPV])
    TK = kb.sb(G, "TK", [128, 96])
    identb = kb.sb(G, "identb", [128, 128], BF16)
    onesm = kb.sb(G, "onesm", [128, 128], BF16)
    onesb = kb.sb(G, "onesb", [128, 128], BF16)
    NW = 4
    wb = [kb.sb(G, f"wb{i}", [128, 16, 128], BF16) for i in range(NW)]
    rstd_bc = kb.sb(G, "rstd_bc", [128, 512])
    pst = [kb.psum(G, f"pst{i}", [128, 1024], BF16) for i in range(2)]
    psW = kb.psum(G, "psW", [128, 1024], F32)
    pg = [kb.psum(G, f"pg{i}", [128, 512], F32) for i in range(4)]
    st = {"w": 0, "p": 0}

    def next_ps():
        st["p"] = (st["p"] + 1) % 3
        return pg[st["p"]]

    psE = pg[3]

    def load_w(src, kc=16):
        i = st["w"] % NW
        st["w"] += 1
        kb.dma([(wb[i].t[:, 0:kc, :], src)], W=[wb[i].b], q="pool")
        return wb[i]

    TBF = [(0, 512), (512, 128)]
    TBP = [(0, 512)]

    def linearT(blocks, srcT, tbs, evac, kc=16):
        for j, src in enumerate(blocks):
            slot = load_w(src, kc)
            for bi, (t0, n) in enumerate(tbs):
                ps = next_ps()
                kb.mm(ps.t[:, 0:n], [(slot.t[:, c, :], srcT.t[:, c, t0:t0 + n]) for c in range(kc)],
                      R=[slot.b, srcT.b], W=[ps.b])
                evac(j, bi, t0, n, ps)

    def rmsnormT(src, ntok, gcol, dst, S, out_f32_dram=None):
        sq = kb.sb(S, "sq", [128, 16, 256], BF16)
        lnv = kb.sb(S, "lnv", [128, 512])
        ost = [kb.sb(S, f"ost{i}", [128, 512]) for i in range(2)] if out_f32_dram is not None else None
        for t0 in range(0, ntok, 256):
            n = min(256, ntok - t0)
            kb.op("act", lambda e: e.activation(out=sq.t[:, :, 0:n], in_=src.t[:, :, t0:t0 + n], func=AF.Square),
                  R=[src.b], W=[sq.b])
            ps = next_ps()
            kb.mm(ps.t[:, 0:n], [(onesm.t[:, :], sq.t[:, c, 0:n]) for c in range(16)], R=[onesm.b, sq.b], W=[ps.b])
            kb.op("act", lambda e: e.activation(out=lnv.t[:, 0:n], in_=ps.t[:, 0:n], func=AF.Ln, bias=EPS),
                  R=[ps.b], W=[lnv.b])
            kb.op("act", lambda e: e.activation(out=rstd_bc.t[:, 0:n], in_=lnv.t[:, 0:n], func=AF.Exp, scale=-0.5),
                  R=[lnv.b], W=[rstd_bc.b])
            for c in range(16):
                if out_f32_dram is None:
                    kb.op("dve", lambda e: e.scalar_tensor_tensor(
                        out=dst.t[:, c, t0:t0 + n], in0=src.t[:, c, t0:t0 + n], scalar=PV.t[:, gcol + c:gcol + c + 1],
                        in1=rstd_bc.t[:, 0:n], op0=ALU.mult, op1=ALU.mult), R=[src.b, PV.b, rstd_bc.b], W=[dst.b])
                else:
                    o = ost[c % 2]
                    kb.op("dve", lambda e: e.scalar_tensor_tensor(
                        out=o.t[:, 0:n], in0=src.t[:, c, t0:t0 + n], scalar=PV.t[:, gcol + c:gcol + c + 1],
                        in1=rstd_bc.t[:, 0:n], op0=ALU.mult, op1=ALU.mult), R=[src.b, PV.b, rstd_bc.b], W=[o.b])
                    kb.dma([(out_f32_dram[c * 128:(c + 1) * 128, t0:t0 + n], o.t[:, 0:n])], R=[o.b])

    kb.dma([(PV.t[:, :], pvec[:, :]), (TK.t[:, :], tokp[:, :])], W=[PV.b, TK.b], owner=PV.b)
    A = ExitStack()
    CST = kb.sb(A, "CST", [128, 784])
    SEL = kb.sb(A, "SEL", [128, 16, 128], BF16)
    T0 = ExitStack()
    SELF = kb.sb(T0, "SELF", [128, 2048])
    kb.dma([(CST.t[:, :], cst[:, 0:784]), (SELF.t[:, :], cst[:, 784:784 + 2048])], W=[CST.b, SELF.b], owner=CST.b)
    kb.op("pool", lambda e: e.tensor_copy(out=SEL.t[:, :, :], in_=SELF.t[:, :].rearrange("p (b l) -> p b l", b=16)),
          R=[SELF.b], W=[SEL.b])
    kb.barrier()
    T0.close()
    kb.op("dve", lambda e: e.tensor_copy(out=identb.t[:, :], in_=CST.t[:, C_ID:C_ID + 128]), R=[CST.b], W=[identb.b])
    kb.op("dve", lambda e: e.memset(onesm.t[:, :], 1.0 / 2048.0), W=[onesm.b])
    kb.op("dve", lambda e: e.memset(onesb.t[:, :], 1.0), W=[onesb.b])
    onesf = kb.sb(A, "onesf", [128, 128])
    kb.op("dve", lambda e: e.memset(onesf.t[:, :], 1.0), W=[onesf.b])
    FL = kb.sb(A, "FL", [128, 1])
    kb.dma([(FL.t[:, :], flag[:, :])], W=[FL.b])
    Abc = kb.sb(A, "Abc", [128, 32])
    kb.op("act", lambda e: e.activation(out=Abc.t[:, :], in_=TK.t[:, 32:64], func=AF.Exp), R=[TK.b], W=[Abc.b])
    kb.op("dve", lambda e: e.tensor_scalar(out=Abc.t[:, :], in0=Abc.t[:, :], scalar1=-1.0, scalar2=None, op0=ALU.mult),
          R=[Abc.b], W=[Abc.b])
    wdtb = kb.sb(A, "wdtb", [128, 16, 32], BF16)
    T0 = ExitStack()
    wdtf = kb.sb(T0, "wdtf", [128, 16, 32])
    kb.dma([(wdtf.t[:, :, :], wdt[:, :, :])], W=[wdtf.b])
    kb.op("dve", lambda e: e.tensor_copy(out=wdtb.t[:, :, :], in_=wdtf.t[:, :, :]), R=[wdtf.b], W=[wdtb.b])
    kb.barrier()
    T0.close()

    STall = kb.sb(A, "STall", [128, 4, 512])
    halo_x = kb.sb(A, "halo_x", [128, 24, 3])
    halo_sc = kb.sb(A, "halo_sc", [128, 16, 2])
    xnl = kb.sb(A, "xnl", [128, 16, 2], BF16)
    kb.op("dve", lambda e: e.memset(STall.t[:, :, :], 0.0), W=[STall.b])
    kb.op("dve", lambda e: e.memset(halo_x.t[:, :, :], 0.0), W=[halo_x.b])
    IOT = kb.sb(A, "IOT", [128, 16])
    kb.dma([(IOT.t[:, :], cst[:, C_IOTA:C_IOTA + 16])], W=[IOT.b])
    xnT = kb.sb(A, "xnT", [128, 16, SEGW], BF16)
    cnt = {"h": 0}

    def dt_pass(ntile):
        for i in range(ntile):
            ps = next_ps()
            kb.mm(ps.t[:, 0:32], [(xnT.t[:, c, i * 128:(i + 1) * 128], wdtb.t[:, c, :]) for c in range(16)],
                  R=[xnT.b, wdtb.b], W=[ps.b])
            kb.op("dve", lambda e: e.tensor_tensor(out=dtt.t[:, i, :], in0=ps.t[:, 0:32], in1=TK.t[:, 0:32], op=ALU.add),
                  R=[ps.b, TK.b], W=[dtt.b])
            kb.op("act", lambda e: e.activation(out=dtt.t[:, i, :], in_=dtt.t[:, i, :], func=AF.Exp), R=[dtt.b], W=[dtt.b])
            kb.op("act", lambda e: e.activation(out=dtt.t[:, i, :], in_=dtt.t[:, i, :], func=AF.Ln, bias=1.0), R=[dtt.b], W=[dtt.b])
            kb.op("dve", lambda e: e.tensor_tensor(out=at.t[:, i, :], in0=dtt.t[:, i, :], in1=Abc.t[:, :], op=ALU.mult),
                  R=[dtt.b, Abc.b], W=[at.b])

    def wout_apply(kblk, tbs):
        def ev(j, bi, t0, n, ps):
            kb.op("dve", lambda e: e.tensor_tensor(out=XT.t[:, j, t0:t0 + n], in0=XT.t[:, j, t0:t0 + n], in1=ps.t[:, 0:n],
                                                    op=ALU.add), R=[XT.b, ps.b], W=[XT.b])
        linearT([wout[kblk, j] for j in range(16)], yTblk, tbs, ev, kc=4)

    def ssd_seg(full, samp_seg, last_seg):
        tbs = TBF if samp_seg else TBP
        for g in range(4):
            g8 = g * 8
            xbc_tiles = [g * 4 + k for k in range(4)] + [16 + g, 20 + g]
            ST = STall.t[:, g, :]

            def evac_x(j, bi, t0, n, ps, xbc_tiles=xbc_tiles):
                tl = xbc_tiles[j]
                cnt["h"] += 1
                a_ = acc[cnt["h"] % 2]
                if t0 < 512:
                    h = hb[cnt["h"] % 2]
                    kb.op("act", lambda e: e.copy(out=h.t[:, 3:3 + n], in_=ps.t[:, 0:n]), R=[ps.b], W=[h.b])
                    kb.op("dve", lambda e: e.tensor_copy(out=h.t[:, 0:3], in_=halo_x.t[:, tl, :]), R=[halo_x.b], W=[h.b])
                    kb.op("dve", lambda e: e.tensor_copy(out=halo_x.t[:, tl, :], in_=h.t[:, 512:515]), R=[h.b], W=[halo_x.b])
                    kb.op("dve", lambda e: e.tensor_scalar(out=a_.t[:, 0:n], in0=h.t[:, 0:n],
                                                            scalar1=PV.t[:, PV_CW + tl * 4:PV_CW + tl * 4 + 1],
                                                            scalar2=PV.t[:, PV_CB + tl:PV_CB + tl + 1], op0=ALU.mult, op1=ALU.add),
                          R=[h.b, PV.b], W=[a_.b])
                    for q in range(1, 4):
                        kb.op("dve", lambda e: e.scalar_tensor_tensor(
                            out=a_.t[:, 0:n], in0=h.t[:, q:q + n], scalar=PV.t[:, PV_CW + tl * 4 + q:PV_CW + tl * 4 + q + 1],
                            in1=a_.t[:, 0:n], op0=ALU.mult, op1=ALU.add), R=[h.b, PV.b, a_.b], W=[a_.b])
                    kb.op("act", lambda e: e.activation(out=cvT.t[:, j, t0:t0 + n], in_=a_.t[:, 0:n], func=AF.Silu),
                          R=[a_.b], W=[cvT.b])
                else:
                    kb.op("act", lambda e: e.copy(out=a_.t[:, 0:128], in_=ps.t[:, 0:128]), R=[ps.b], W=[a_.b])
                    kb.op("dve", lambda e: e.tensor_copy(out=hs.t[:, :, 3:11], in_=a_.t[:, 0:128].rearrange("p (b t) -> p b t", b=16)),
                          R=[a_.b], W=[hs.b])
                    kb.op("dve", lambda e: e.tensor_copy(out=hs.t[:, :, 0:3],
                                                          in_=histx.t[:, tl, :].rearrange("p (b t) -> p b t", b=16)),
                          R=[histx.b], W=[hs.b])
                    kb.op("dve", lambda e: e.tensor_copy(out=oconv_s.t[:, tl, :, :], in_=hs.t[:, :, 8:11]), R=[hs.b], W=[oconv_s.b])
                    av = a_.t[:, 0:128].rearrange("p (b t) -> p b t", b=16)
                    kb.op("dve", lambda e: e.tensor_scalar(out=av, in0=hs.t[:, :, 0:8],
                                                            scalar1=PV.t[:, PV_CW + tl * 4:PV_CW + tl * 4 + 1],
                                                            scalar2=PV.t[:, PV_CB + tl:PV_CB + tl + 1], op0=ALU.mult, op1=ALU.add),
                          R=[hs.b, PV.b], W=[a_.b])
                    for q in range(1, 4):
                        kb.op("dve", lambda e: e.scalar_tensor_tensor(
                            out=av, in0=hs.t[:, :, q:q + 8], scalar=PV.t[:, PV_CW + tl * 4 + q:PV_CW + tl * 4 + q + 1],
                            in1=av, op0=ALU.mult, op1=ALU.add), R=[hs.b, PV.b, a_.b], W=[a_.b])
                    kb.op("act", lambda e: e.activation(out=cvT.t[:, j, 512:640], in_=a_.t[:, 0:128], func=AF.Silu),
                          R=[a_.b], W=[cvT.b])

            linearT([wxbc[t] for t in xbc_tiles], xnT, tbs, evac_x)
            if KSUB == 1:
                return
            if full:
                def evac_z(j, bi, t0, n, ps):
                    kb.op("act", lambda e: e.activation(out=szT.t[:, j, t0:t0 + n], in_=ps.t[:, 0:n], func=AF.Silu),
                          R=[ps.b], W=[szT.b])
                linearT([wz[g * 4 + k] for k in range(4)], xnT, tbs, evac_z)
            kb.op("act", lambda e: e.copy(out=STb.t[:, :], in_=ST), R=[STall.b], W=[STb.b])

            nchunk = 5 if samp_seg else 4
            for ci in range(nchunk):
                samp = ci == 4
                c0 = ci * 128
                cols = slice(c0, c0 + 128)
                trm = CST.t[:, C_TRS:C_TRS + 128] if samp else CST.t[:, C_TRP:C_TRP + 128]
                ngm = CST.t[:, C_NGS:C_NGS + 128] if samp else CST.t[:, C_NGP:C_NGP + 128]
                totm = CST.t[:, C_TOTS:C_TOTS + 128] if samp else onesf.t[:, :]
                kb.tr([(pst[0].t[:, k * 128:(k + 1) * 128], cvT.t[:, k, cols]) for k in range(5)], identb.t[:, :],
                      R=[cvT.b, identb.b], W=[pst[0].b])
                kb.op("act", lambda e: e.copy(out=xstok.t[:, :], in_=pst[0].t[:, 0:512]), R=[pst[0].b], W=[xstok.b])
                kb.op("dve", lambda e: e.tensor_tensor(out=xdt.t[:, :].rearrange("p (r d) -> p r d", r=8),
                                                        in0=xstok.t[:, :].rearrange("p (r d) -> p r d", r=8),
                                                        in1=bl(dtt.t[:, ci, g8:g8 + 8], 8, 64), op=ALU.mult),
                      R=[xstok.b, dtt.b], W=[xdt.b])
                kb.op("act", lambda e: e.copy(out=Btok.t[:, :], in_=pst[0].t[:, 512:640]), R=[pst[0].b], W=[Btok.b])
                if KSUB == 2:
                    return
                if full:
                    kb.tr([(pst[1].t[:, k * 128:(k + 1) * 128], szT.t[:, k, cols]) for k in range(4)], identb.t[:, :],
                          R=[szT.b, identb.b], W=[pst[1].b])
                    kb.op("act", lambda e: e.copy(out=sztok.t[:, :], in_=pst[1].t[:, 0:512]), R=[pst[1].b], W=[sztok.b])
                psA = next_ps()
                kb.mm(psA.t[:, 0:8], [(trm, at.t[:, ci, g8:g8 + 8])], R=[CST.b, at.b], W=[psA.b])
                kb.mm(psA.t[:, 8:16], [(totm, at.t[:, ci, g8:g8 + 8])], R=[CST.b, onesf.b, at.b], W=[psA.b])
                kb.op("act", lambda e: e.copy(out=sm.t[:, 0:8], in_=psA.t[:, 0:8]), R=[psA.b], W=[sm.b])
                kb.op("dve", lambda e: e.tensor_tensor(out=sm.t[:, 8:16], in0=psA.t[:, 8:16], in1=sm.t[:, 0:8], op=ALU.subtract),
                      R=[psA.b, sm.b], W=[sm.b])
                kb.op("act", lambda e: e.activation(out=sm.t[:, 16:24], in_=sm.t[:, 8:16], func=AF.Exp), R=[sm.b], W=[sm.b])
                kb.op("act", lambda e: e.activation(out=sm.t[:, 24:32], in_=psA.t[:, 8:16], func=AF.Exp), R=[psA.b], W=[sm.b])
                kb.op("dve", lambda e: e.tensor_tensor(out=xdtd.t[:, :].rearrange("p (r d) -> p r d", r=8),
                                                        in0=xdt.t[:, :].rearrange("p (r d) -> p r d", r=8),
                                                        in1=bl(sm.t[:, 16:24], 8, 64), op=ALU.mult),
                      R=[xdt.b, sm.b], W=[xdtd.b])
                if KSUB == 3:
                    return
                if full:
                    kb.op("dve", lambda e: e.tensor_tensor(out=rhs_bc.t[:, :, :], in0=bm(trm, 8, 128),
                                                            in1=bl(at.t[:, ci, g8:g8 + 8], 8, 128), op=ALU.mult),
                          R=[CST.b, at.b], W=[rhs_bc.b])
                    for hf in range(2):
                        kb.mm(psW.t[:, hf * 512:(hf + 1) * 512],
                              [(onesf.t[:, :], rhs_bc.t[:, hf * 4:(hf + 1) * 4, :].rearrange("p r l -> p (r l)"))],
                              R=[onesf.b, rhs_bc.b], W=[psW.b])
                    kb.op("act", lambda e: e.copy(out=T1.t[:, :, :].rearrange("p r l -> p (r l)"), in_=psW.t[:, :]), R=[psW.b], W=[T1.b])
                    kb.op("dve", lambda e: e.tensor_tensor(out=T1.t[:, :, :], in0=T1.t[:, :, :],
                                                            in1=bl(sm.t[:, 0:8], 8, 128), op=ALU.subtract),
                          R=[T1.b, sm.b], W=[T1.b])
                    kb.op("dve", lambda e: e.tensor_tensor(out=T1.t[:, :, :], in0=T1.t[:, :, :], in1=bm(ngm, 8, 128), op=ALU.add),
                          R=[T1.b, CST.b], W=[T1.b])
                    kb.op("act", lambda e: e.activation(out=Eb.t[:, :, :], in_=T1.t[:, :, :], func=AF.Exp), R=[T1.b], W=[Eb.b])
                    psC = next_ps()
                    kb.mm(psC.t[:, 0:128], [(cvT.t[:, 4, cols], cvT.t[:, 5, cols])], R=[cvT.b], W=[psC.b])
                    kb.op("act", lambda e: e.copy(out=cbS.t[:, :], in_=psC.t[:, 0:128]), R=[psC.b], W=[cbS.b])
                    kb.op("dve", lambda e: e.tensor_tensor(out=Mb.t[:, :, :], in0=Eb.t[:, :, :], in1=bm(cbS.t[:, :], 8, 128),
                                                            op=ALU.mult), R=[Eb.b, cbS.b], W=[Mb.b])
                    if not samp:
                        kb.mm(psE.t[:, :], [(cvT.t[:, 5, cols], STb.t[:, :])], R=[cvT.b, STb.b], W=[psE.b])
                    else:
                        kb.op("dve", lambda e: e.tensor_tensor(out=CTz.t[:, :, :], in0=bm(cvT.t[:, 5, cols], 16, 128),
                                                                in1=SEL.t[:, :, :], op=ALU.mult), R=[cvT.b, SEL.b], W=[CTz.b])
                        kb.op("dve", lambda e: e.tensor_tensor(out=rhs_s.t[:, :, :], in0=bm(at.t[:, 4, g8:g8 + 8], 16, 8),
                                                                in1=bl(CST.t[:, C_RM:C_RM + 16], 16, 8), op=ALU.mult),
                              R=[at.b, CST.b], W=[rhs_s.b])
                        psT_ = next_ps()
                        kb.mm(psT_.t[:, 0:128], [(onesf.t[:, :], rhs_s.t[:, :, :].rearrange("p b r -> p (b r)"))],
                              R=[onesf.b, rhs_s.b], W=[psT_.b])
                        kb.op("act", lambda e: e.activation(out=dch_s.t[:, :, :].rearrange("p b r -> p (b r)"),
                                                             in_=psT_.t[:, 0:128], func=AF.Exp), R=[psT_.b], W=[dch_s.b])
                        for b in range(16):
                            sf, sbb = stf[b % 2], stb[b % 2]
                            kb.dma([(sf.t[:, :], stT[b, :, g * 512:(g + 1) * 512])], W=[sf.b])
                            kb.op("act", lambda e: e.copy(out=sbb.t[:, :], in_=sf.t[:, :]), R=[sf.b], W=[sbb.b])
                            kb.mm(psE.t[:, :], [(CTz.t[:, b, :], sbb.t[:, :])], R=[CTz.b, sbb.b], W=[psE.b],
                                  start=(b == 0), stop=(b == 15))
                            bz = Bz[b % 2]
                            kb.op("dve", lambda e: e.tensor_scalar(out=bz.t[:, :], in0=Btok.t[:, :],
                                                                    scalar1=CST.t[:, C_RM + b:C_RM + b + 1], scalar2=None,
                                                                    op0=ALU.mult), R=[Btok.b, CST.b], W=[bz.b])
                            psF = next_ps()
                            kb.mm(psF.t[:, :], [(bz.t[:, :], xdtd.t[:, :])], R=[bz.b, xdtd.b], W=[psF.b])
                            o_ = so[b % 2]
                            kb.op("dve", lambda e: e.tensor_tensor(out=o_.t[:, :].rearrange("p (r d) -> p r d", r=8),
                                                                    in0=sf.t[:, :].rearrange("p (r d) -> p r d", r=8),
                                                                    in1=bl(dch_s.t[:, b, :], 8, 64), op=ALU.mult),
                                  R=[sf.b, dch_s.b], W=[o_.b])
                            kb.op("dve", lambda e: e.tensor_tensor(out=o_.t[:, :], in0=o_.t[:, :], in1=psF.t[:, :], op=ALU.add),
                                  R=[o_.b, psF.b], W=[o_.b])
                            kb.dma([(o_sssm[b, :, g * 512:(g + 1) * 512], o_.t[:, :])], R=[o_.b])
                    psD = next_ps()
                    for r in range(8):
                        kb.mm(psD.t[:, r * 64:(r + 1) * 64], [(Mb.t[:, r, :], xdt.t[:, r * 64:(r + 1) * 64])],
                              R=[Mb.b, xdt.b], W=[psD.b])
                    kb.op("act", lambda e: e.activation(out=sm.t[:, 32:40], in_=sm.t[:, 0:8], func=AF.Exp), R=[sm.b], W=[sm.b])
                    kb.op("act", lambda e: e.copy(out=y1.t[:, :], in_=psE.t[:, :]), R=[psE.b], W=[y1.b])
                    kb.op("dve", lambda e: e.tensor_tensor(out=y1.t[:, :].rearrange("p (r d) -> p r d", r=8),
                                                            in0=y1.t[:, :].rearrange("p (r d) -> p r d", r=8),
                                                            in1=bl(sm.t[:, 32:40], 8, 64), op=ALU.mult),
                          R=[y1.b, sm.b], W=[y1.b])
                    kb.op("dve", lambda e: e.tensor_tensor(out=y1.t[:, :], in0=y1.t[:, :], in1=psD.t[:, :], op=ALU.add),
                          R=[y1.b, psD.b], W=[y1.b])
                    kb.op("dve", lambda e: e.tensor_tensor(out=y2.t[:, :].rearrange("p (r d) -> p r d", r=8),
                                                            in0=xstok.t[:, :].rearrange("p (r d) -> p r d", r=8),
                                                            in1=bl(TK.t[:, 64 + g8:64 + g8 + 8], 8, 64), op=ALU.mult),
                          R=[xstok.b, TK.b], W=[y2.b])
                    kb.op("dve", lambda e: e.tensor_tensor(out=y1.t[:, :], in0=y1.t[:, :], in1=y2.t[:, :], op=ALU.add),
                          R=[y1.b, y2.b], W=[y1.b])
                    kb.op("dve", lambda e: e.tensor_tensor(out=y1.t[:, :], in0=y1.t[:, :], in1=sztok.t[:, :], op=ALU.mult),
                          R=[y1.b, sztok.b], W=[y1.b])
                    kb.op("act", lambda e: e.activation(out=y2.t[:, :], in_=y1.t[:, :], func=AF.Square, accum_out=sm.t[:, 40:41]),
                          R=[y1.b], W=[y2.b, sm.b])
                    kb.op("act", lambda e: e.activation(out=sm.t[:, 41:42], in_=sm.t[:, 40:41], func=AF.Ln, bias=EPS, scale=1.0 / 512.0),
                          R=[sm.b], W=[sm.b])
                    kb.op("act", lambda e: e.activation(out=sm.t[:, 42:43], in_=sm.t[:, 41:42], func=AF.Exp, scale=-0.5),
                          R=[sm.b], W=[sm.b])
                    kb.op("dve", lambda e: e.tensor_scalar(out=ynb.t[:, :], in0=y1.t[:, :], scalar1=sm.t[:, 42:43], scalar2=None,
                                                            op0=ALU.mult), R=[y1.b, sm.b], W=[ynb.b])
                    kb.tr([(pst[1].t[:, k * 128:(k + 1) * 128], ynb.t[:, k * 128:(k + 1) * 128]) for k in range(4)], identb.t[:, :],
                          R=[ynb.b, identb.b], W=[pst[1].b])
                    for k in range(4):
                        kb.op("dve", lambda e: e.tensor_scalar(out=yTblk.t[:, k, cols], in0=pst[1].t[:, k * 128:(k + 1) * 128],
                                                                scalar1=PV.t[:, PV_SSDN + g * 4 + k:PV_SSDN + g * 4 + k + 1],
                                                                scalar2=None, op0=ALU.mult), R=[pst[1].b, PV.b], W=[yTblk.b])
                if not samp:
                    psF = next_ps()
                    kb.mm(psF.t[:, :], [(Btok.t[:, :], xdtd.t[:, :])], R=[Btok.b, xdtd.b], W=[psF.b])
                    kb.op("dve", lambda e: e.tensor_tensor(out=ST.rearrange("p (r d) -> p r d", r=8),
                                                            in0=ST.rearrange("p (r d) -> p r d", r=8),
                                                            in1=bl(sm.t[:, 24:32], 8, 64), op=ALU.mult),
                          R=[STall.b, sm.b], W=[STall.b])
                    kb.op("dve", lambda e: e.tensor_tensor(out=ST, in0=ST, in1=psF.t[:, :], op=ALU.add),
                          R=[STall.b, psF.b], W=[STall.b])
                    kb.op("act", lambda e: e.copy(out=STb.t[:, :], in_=ST), R=[STall.b], W=[STb.b])
                if KSUB == 4:
                    return
            if full and KDBG and samp_seg and g == 0:
                kb.op("dve", lambda e: e.tensor_copy(out=y1.t[:, :].rearrange("p (a b) -> p a b", a=4), in_=yTblk.t[:, :, 512:640]),
                      R=[yTblk.b], W=[y1.b])
                kb.dma([(dbg1[:, :], y1.t[:, :])], R=[y1.b])
            if full:
                wout_apply(g, tbs)

    def sc_seg(first_main, samp_seg):
        tbs = TBF if samp_seg else TBP
        for j in range(16):
            sh = load_w(wsch[j])
            sc_ = load_w(wscc[j])
            if first_main:
                p1 = next_ps()
                kb.mm(p1.t[:, 0:2], [(sh.t[:, c, :], xnl.t[:, c, :]) for c in range(16)], R=[sh.b, xnl.b], W=[p1.b])
                kb.mm(p1.t[:, 2:4], [(sc_.t[:, c, :], xnl.t[:, c, :]) for c in range(16)], R=[sc_.b, xnl.b], W=[p1.b])
                kb.op("act", lambda e: e.copy(out=hl.t[:, :], in_=p1.t[:, 0:2]), R=[p1.b], W=[hl.b])
                kb.op("dve", lambda e: e.tensor_tensor(out=halo_sc.t[:, j, :], in0=hl.t[:, :], in1=p1.t[:, 2:4], op=ALU.mult),
                      R=[hl.b, p1.b], W=[halo_sc.b])
            phs, pcs = [], []
            for bi, (t0, n) in enumerate(tbs):
                ph, pc = next_ps(), None
                kb.mm(ph.t[:, 0:n], [(sh.t[:, c, :], xnT.t[:, c, t0:t0 + n]) for c in range(16)], R=[sh.b, xnT.b], W=[ph.b])
                kb.op("act", lambda e: e.copy(out=hsb.t[:, 0:n], in_=ph.t[:, 0:n]), R=[ph.b], W=[hsb.b])
                pc = next_ps()
                kb.mm(pc.t[:, 0:n], [(sc_.t[:, c, :], xnT.t[:, c, t0:t0 + n]) for c in range(16)], R=[sc_.b, xnT.b], W=[pc.b])
                cnt["h"] += 1
                a_ = acc[cnt["h"] % 2]
                if t0 < 512:
                    p_ = pb[cnt["h"] % 2]
                    kb.op("dve", lambda e: e.tensor_tensor(out=p_.t[:, 2:2 + n], in0=hsb.t[:, 0:n], in1=pc.t[:, 0:n], op=ALU.mult),
                          R=[hsb.b, pc.b], W=[p_.b])
                    kb.op("dve", lambda e: e.tensor_copy(out=p_.t[:, 0:2], in_=halo_sc.t[:, j, :]), R=[halo_sc.b], W=[p_.b])
                    kb.op("dve", lambda e: e.tensor_copy(out=halo_sc.t[:, j, :], in_=p_.t[:, 512:514]), R=[p_.b], W=[halo_sc.b])
                    kb.op("dve", lambda e: e.tensor_scalar(out=a_.t[:, 0:n], in0=p_.t[:, 0:n],
                                                            scalar1=PV.t[:, PV_SCW + j * 3:PV_SCW + j * 3 + 1], scalar2=None,
                                                            op0=ALU.mult), R=[p_.b, PV.b], W=[a_.b])
                    for q in range(1, 3):
                        kb.op("dve", lambda e: e.scalar_tensor_tensor(
                            out=a_.t[:, 0:n], in0=p_.t[:, q:q + n], scalar=PV.t[:, PV_SCW + j * 3 + q:PV_SCW + j * 3 + q + 1],
                            in1=a_.t[:, 0:n], op0=ALU.mult, op1=ALU.add), R=[p_.b, PV.b, a_.b], W=[a_.b])
                else:
                    kb.op("act", lambda e: e.copy(out=a_.t[:, 0:128], in_=pc.t[:, 0:128]), R=[pc.b], W=[a_.b])
                    kb.op("dve", lambda e: e.tensor_tensor(out=pbs.t[:, :, 2:10],
                                                            in0=hsb.t[:, 0:128].rearrange("p (b t) -> p b t", b=16),
                                                            in1=a_.t[:, 0:128].rearrange("p (b t) -> p b t", b=16), op=ALU.mult),
                          R=[hsb.b, a_.b], W=[pbs.b])
                    kb.op("dve", lambda e: e.tensor_copy(out=pbs.t[:, :, 0:2], in_=hists.t[:, j, :].rearrange("p (b t) -> p b t", b=16)),
                          R=[hists.b], W=[pbs.b])
                    kb.op("dve", lambda e: e.tensor_copy(out=osc_s.t[:, j, :, :], in_=pbs.t[:, :, 8:10]), R=[pbs.b], W=[osc_s.b])
                    av = a_.t[:, 0:128].rearrange("p (b t) -> p b t", b=16)
                    kb.op("dve", lambda e: e.tensor_scalar(out=av, in0=pbs.t[:, :, 0:8],
                                                            scalar1=PV.t[:, PV_SCW + j * 3:PV_SCW + j * 3 + 1], scalar2=None,
                                                            op0=ALU.mult), R=[pbs.b, PV.b], W=[a_.b])
                    for q in range(1, 3):
                        kb.op("dve", lambda e: e.scalar_tensor_tensor(
                            out=av, in0=pbs.t[:, :, q:q + 8], scalar=PV.t[:, PV_SCW + j * 3 + q:PV_SCW + j * 3 + q + 1],
                            in1=av, op0=ALU.mult, op1=ALU.add), R=[pbs.b, PV.b, a_.b], W=[a_.b])
                kb.op("dve", lambda e: e.tensor_copy(out=yTblk.t[:, j % 4, t0:t0 + n], in_=a_.t[:, 0:n]), R=[a_.b], W=[yTblk.b])
            sb_ = load_w(wscb[j])
            for bi, (t0, n) in enumerate(tbs):
                pbb = next_ps()
                kb.mm(pbb.t[:, 0:n], [(sb_.t[:, c, :], xnT.t[:, c, t0:t0 + n]) for c in range(16)], R=[sb_.b, xnT.b], W=[pbb.b])
                kb.op("dve", lambda e: e.tensor_tensor(out=yTblk.t[:, j % 4, t0:t0 + n], in0=yTblk.t[:, j % 4, t0:t0 + n],
                                                        in1=pbb.t[:, 0:n], op=ALU.mult), R=[yTblk.b, pbb.b], W=[yTblk.b])
            if KDBG and samp_seg and j == 3:
                kb.op("dve", lambda e: e.tensor_copy(out=hsb.t[:, :].rearrange("p (a b) -> p a b", a=4), in_=yTblk.t[:, :, 512:640]),
                      R=[yTblk.b], W=[hsb.b])
                kb.dma([(dbg2[:, :], hsb.t[:, :])], R=[hsb.b])
            if j % 4 == 3:
                wout_apply(4 + j // 4, tbs)


    SCALE = 1.0 / math.sqrt(512.0)

    def attention_seg(first, samp_seg, ntok):
        tbs = TBF if samp_seg else TBP
        AT = ExitStack()
        kTp = kb.sb(AT, "kTp", [128, 16, 256], BF16)
        vp = kb.sb(AT, "vp", [128, 2, 2048], BF16)
        if True:
            KV = ExitStack()
            MT = kb.sb(KV, "MT", [128, 16, 256])
            mnT = kb.sb(KV, "mnT", [128, 16, 256], BF16)
            vTb = kb.sb(KV, "vTb", [128, 16, 256], BF16)
            stg = [kb.sb(KV, f"stg{i}", [128, 256]) for i in range(2)]
            kb.dma([(MT.t[:, 4 * i:4 * i + 4, :], memT[512 * i:512 * (i + 1), :].rearrange("(c p) t -> p c t", p=128))
                    for i in range(4)], W=[MT.b])
            S_ = ExitStack()
            rmsnormT(MT, 256, PV_GMKV, mnT, S_)
            kb.barrier()
            S_.close()

            def mk_ev(dstT, odram):
                def ev(j, bi, t0, n, ps):
                    o = stg[j % 2]
                    kb.op("act", lambda e: e.copy(out=o.t[:, :], in_=ps.t[:, 0:256]), R=[ps.b], W=[o.b])
                    kb.op("dve", lambda e: e.tensor_copy(out=dstT.t[:, j, :], in_=o.t[:, :]), R=[o.b], W=[dstT.b])
                    if first:
                        kb.dma([(odram[j * 128:(j + 1) * 128, :], o.t[:, :])], R=[o.b])
                return ev
            linearT([wmk[j] for j in range(16)], mnT, [(0, 256)], mk_ev(kTp, o_mk))
            linearT([wmv[j] for j in range(16)], mnT, [(0, 256)], mk_ev(vTb, o_mv))
            for j in range(16):
                pt = pst[j % 2]
                kb.tr([(pt.t[:, mt * 128:(mt + 1) * 128], vTb.t[:, j, mt * 128:(mt + 1) * 128]) for mt in range(2)],
                      identb.t[:, :], R=[vTb.b, identb.b], W=[pt.b])
                for mt in range(2):
                    kb.op("act", lambda e: e.copy(out=vp.t[:, mt, j * 128:(j + 1) * 128], in_=pt.t[:, mt * 128:(mt + 1) * 128]),
                          R=[pt.b], W=[vp.b])
            kb.barrier()
            KV.close()
        qT = kb.sb(AT, "qT", [128, 16, SEGW], BF16)
        oT = xnT
        eT = [kb.sb(AT, f"eT{i}", [128, 512], BF16) for i in range(2)]
        rz = kb.sb(AT, "rz", [128, 512])
        S_ = ExitStack()
        rmsnormT(XT, ntok, PV_GMQ, xnT, S_)
        kb.barrier()
        S_.close()

        def ev_q(j, bi, t0, n, ps):
            kb.op("act", lambda e: e.copy(out=qT.t[:, j, t0:t0 + n], in_=ps.t[:, 0:n]), R=[ps.b], W=[qT.b])
        linearT([wmq[j] for j in range(16)], xnT, tbs, ev_q)
        for h in range(4):
            for mt in range(2):
                ps = next_ps()
                kb.mm(ps.t[:, 0:512], [(kTp.t[:, h * 4 + dc, mt * 128:(mt + 1) * 128], qT.t[:, h * 4 + dc, 0:512]) for dc in range(4)],
                      R=[kTp.b, qT.b], W=[ps.b])
                kb.op("act", lambda e: e.activation(out=eT[mt].t[:, :], in_=ps.t[:, 0:512], func=AF.Exp, scale=SCALE),
                      R=[ps.b], W=[eT[mt].b])
            ps = next_ps()
            kb.mm(ps.t[:, 0:512], [(onesb.t[:, :], eT[mt].t[:, :]) for mt in range(2)], R=[onesb.b, eT[0].b, eT[1].b], W=[ps.b])
            kb.op("dve", lambda e: e.reciprocal(out=rz.t[:, :], in_=ps.t[:, 0:512]), R=[ps.b], W=[rz.b])
            for dvt in range(4):
                ps = next_ps()
                c_ = h * 512 + dvt * 128
                kb.mm(ps.t[:, 0:512], [(vp.t[:, mt, c_:c_ + 128], eT[mt].t[:, :]) for mt in range(2)],
                      R=[vp.b, eT[0].b, eT[1].b], W=[ps.b])
                kb.op("dve", lambda e: e.tensor_tensor(out=oT.t[:, h * 4 + dvt, 0:512], in0=ps.t[:, 0:512], in1=rz.t[:, :], op=ALU.mult),
                      R=[ps.b, rz.b], W=[oT.b])
        if samp_seg:
            kbf = [kb.sb(AT, f"kbf{i}", [128, 4, 256], BF16) for i in range(4)]
            vbf = [kb.sb(AT, f"vbf{i}", [128, 2, 512], BF16) for i in range(4)]
            es_ = kb.sb(AT, "es_", [128, 16], BF16)
            rzs = kb.sb(AT, "rzs", [128, 8])
            it = 0
            for b in range(16):
                tc0 = 512 + 8 * b
                for h in range(4):
                    i = it % 4
                    it += 1
                    kb.dma([(kbf[i].t[:, :, :], ckT[b, h * 512:(h + 1) * 512, :].rearrange("(c p) m -> p c m", p=128))], W=[kbf[i].b],
                           q="pool")
                    kb.dma([(vbf[i].t[:, :, :], cv[b, :, h * 512:(h + 1) * 512].rearrange("(mt p) d -> p mt d", p=128))], W=[vbf[i].b],
                           q="pool")
                    ps = next_ps()
                    for mt in range(2):
                        kb.mm(ps.t[:, mt * 8:(mt + 1) * 8],
                              [(kbf[i].t[:, dc, mt * 128:(mt + 1) * 128], qT.t[:, h * 4 + dc, tc0:tc0 + 8]) for dc in range(4)],
                              R=[kbf[i].b, qT.b], W=[ps.b])
                    kb.op("act", lambda e: e.activation(out=es_.t[:, :], in_=ps.t[:, 0:16], func=AF.Exp, scale=SCALE),
                          R=[ps.b], W=[es_.b])
                    ps2 = next_ps()
                    kb.mm(ps2.t[:, 0:8], [(onesb.t[:, :], es_.t[:, mt * 8:(mt + 1) * 8]) for mt in range(2)],
                          R=[onesb.b, es_.b], W=[ps2.b])
                    for dvt in range(4):
                        kb.mm(ps2.t[:, 8 + dvt * 8:16 + dvt * 8],
                              [(vbf[i].t[:, mt, dvt * 128:(dvt + 1) * 128], es_.t[:, mt * 8:(mt + 1) * 8]) for mt in range(2)],
                              R=[vbf[i].b, es_.b], W=[ps2.b])
                    kb.op("dve", lambda e: e.reciprocal(out=rzs.t[:, :], in_=ps2.t[:, 0:8]), R=[ps2.b], W=[rzs.b])
                    for dvt in range(4):
                        kb.op("dve", lambda e: e.tensor_tensor(out=oT.t[:, h * 4 + dvt, tc0:tc0 + 8], in0=ps2.t[:, 8 + dvt * 8:16 + dvt * 8],
                                                                in1=rzs.t[:, :], op=ALU.mult), R=[ps2.b, rzs.b], W=[oT.b])

        def ev_o(j, bi, t0, n, ps):
            kb.op("dve", lambda e: e.tensor_tensor(out=XT.t[:, j, t0:t0 + n], in0=XT.t[:, j, t0:t0 + n], in1=ps.t[:, 0:n],
                                                    op=ALU.add), R=[XT.b, ps.b], W=[XT.b])
        linearT([wmo[j] for j in range(16)], oT, tbs, ev_o)
        kb.barrier()
        AT.close()

    def peer_seg(first, samp_seg, ntok):
        tbs = TBF if samp_seg else TBP
        ntile = ntok // 128
        P0 = ExitStack()
        asel_all = kb.sb(P0, "asel_all", [128, 5, 128])
        bsel_all = kb.sb(P0, "bsel_all", [128, 5, 128])
        gts_all = kb.sb(P0, "gts_all", [128, 5, 128])
        PS_ = ExitStack()
        skb = kb.sb(PS_, "skb", [128, 16, 128], BF16)
        T_ = ExitStack()
        skf = kb.sb(T_, "skf", [128, 16, 128])
        kb.dma([(skf.t[:, :, :], skT.rearrange("a p k -> p a k"))], W=[skf.b])
        kb.op("dve", lambda e: e.tensor_copy(out=skb.t[:, :, :], in_=skf.t[:, :, :]), R=[skf.b], W=[skb.b])
        kb.barrier()
        T_.close()
        qpT = kb.sb(PS_, "qpT", [128, 16, SEGW], BF16)
        S_ = ExitStack()
        rmsnormT(XT, ntok, PV_GFFN, xnT, S_)
        kb.barrier()
        S_.close()

        def ev_q(j, bi, t0, n, ps):
            kb.op("act", lambda e: e.copy(out=qpT.t[:, j, t0:t0 + n], in_=ps.t[:, 0:n]), R=[ps.b], W=[qpT.b])
        linearT([wpq[j] for j in range(16)], xnT, tbs, ev_q)
        Ssc = kb.sb(PS_, "Ssc", [128, 16, 128])
        Sw = kb.sb(PS_, "Sw", [128, 256])
        V1 = kb.sb(PS_, "V1", [128, 16, 16])
        I1 = kb.sb(PS_, "I1", [128, 16, 16], U32)
        I1f = kb.sb(PS_, "I1f", [128, 16, 16])
        comb = kb.sb(PS_, "comb", [128, 8, 256])
        Fv = kb.sb(PS_, "Fv", [128, 8, 16])
        PI = kb.sb(PS_, "PI", [128, 8, 16], U32)
        PJ = kb.sb(PS_, "PJ", [128, 8, 16], U32)
        PIf = kb.sb(PS_, "PIf", [128, 8, 16])
        jf = kb.sb(PS_, "jf", [128, 8, 16])
        jpf = kb.sb(PS_, "jpf", [128, 8, 16])
        oh = kb.sb(PS_, "oh", [128, 16, 16])
        gsm = kb.sb(PS_, "gsm", [128, 8])
        IO128 = kb.sb(PS_, "IO128", [128, 128])
        kb.dma([(IO128.t[:, :], cst[:, C_IO128:C_IO128 + 128])], W=[IO128.b])
        trb = kb.sb(PS_, "trb", [128, 3, 128], BF16)
        abT = kb.sb(PS_, "abT", [128, 3, 128])
        OHB = kb.sb(PS_, "OHB", [128, 32, 128], BF16)
        OHA = kb.sb(PS_, "OHA", [128, 32, 128], BF16)
        GS = kb.sb(PS_, "GS", [128, 128, 128], BF16)
        for tt in range(ntile):
            cols = slice(tt * 128, (tt + 1) * 128)
            asel = asel_all.t[:, tt, :].rearrange("p (a b) -> p a b", a=8)
            bsel = bsel_all.t[:, tt, :].rearrange("p (a b) -> p a b", a=8)
            gts = gts_all.t[:, tt, :].rearrange("p (a b) -> p a b", a=8)
            for q4 in range(4):
                ps = next_ps()
                for k in range(4):
                    hi = q4 * 4 + k
                    kb.mm(ps.t[:, k * 128:(k + 1) * 128], [(qpT.t[:, hi, cols], skb.t[:, hi, :])], R=[qpT.b, skb.b], W=[ps.b])
                kb.op("act", lambda e: e.copy(out=Ssc.t[:, q4 * 4:(q4 + 1) * 4, :].rearrange("p a b -> p (a b)"), in_=ps.t[:, 0:512]),
                      R=[ps.b], W=[Ssc.b])
            for hi in range(16):
                kb.op("dve", lambda e: e.max(out=V1.t[:, hi, 0:8], in_=Ssc.t[:, hi, :]), R=[Ssc.b], W=[V1.b])
                kb.op("dve", lambda e: e.max_index(out=I1.t[:, hi, 0:8], in_max=V1.t[:, hi, 0:8], in_values=Ssc.t[:, hi, :]),
                      R=[V1.b, Ssc.b], W=[I1.b])
                kb.op("dve", lambda e: e.match_replace(out=Sw.t[:, 0:128], in_to_replace=V1.t[:, hi, 0:8], in_values=Ssc.t[:, hi, :],
                                                        imm_value=-1e30), R=[V1.b, Ssc.b], W=[Sw.b])
                kb.op("dve", lambda e: e.max(out=V1.t[:, hi, 8:16], in_=Sw.t[:, 0:128]), R=[Sw.b], W=[V1.b])
                kb.op("dve", lambda e: e.max_index(out=I1.t[:, hi, 8:16], in_max=V1.t[:, hi, 8:16], in_values=Sw.t[:, 0:128]),
                      R=[V1.b, Sw.b], W=[I1.b])
            kb.op("dve", lambda e: e.tensor_copy(out=I1f.t[:, :, :], in_=I1.t[:, :, :]), R=[I1.b], W=[I1f.b])
            for h in range(8):
                kb.op("dve", lambda e: e.tensor_tensor(out=comb.t[:, h, :].rearrange("p (a b) -> p a b", a=16),
                                                        in0=bl(V1.t[:, 2 * h, :], 16, 16), in1=bm(V1.t[:, 2 * h + 1, :], 16, 16),
                                                        op=ALU.add), R=[V1.b], W=[comb.b])
                kb.op("dve", lambda e: e.max(out=Fv.t[:, h, 0:8], in_=comb.t[:, h, :]), R=[comb.b], W=[Fv.b])
                kb.op("dve", lambda e: e.max_index(out=PI.t[:, h, 0:8], in_max=Fv.t[:, h, 0:8], in_values=comb.t[:, h, :]),
                      R=[Fv.b, comb.b], W=[PI.b])
                kb.op("dve", lambda e: e.match_replace(out=Sw.t[:, :], in_to_replace=Fv.t[:, h, 0:8], in_values=comb.t[:, h, :],
                                                        imm_value=-1e30), R=[Fv.b, comb.b], W=[Sw.b])
                kb.op("dve", lambda e: e.max(out=Fv.t[:, h, 8:16], in_=Sw.t[:, :]), R=[Sw.b], W=[Fv.b])
                kb.op("dve", lambda e: e.max_index(out=PI.t[:, h, 8:16], in_max=Fv.t[:, h, 8:16], in_values=Sw.t[:, :]),
                      R=[Fv.b, Sw.b], W=[PI.b])
            kb.op("dve", lambda e: e.tensor_tensor(out=gts, in0=Fv.t[:, :, :], in1=bl(Fv.t[:, :, 0], 8, 16), op=ALU.subtract),
                  R=[Fv.b], W=[gts_all.b])
            for h in range(8):
                kb.op("act", lambda e: e.activation(out=gts[:, h, :], in_=gts[:, h, :], func=AF.Exp, accum_out=gsm.t[:, h:h + 1]),
                      R=[gts_all.b], W=[gts_all.b, gsm.b])
            kb.op("dve", lambda e: e.reciprocal(out=gsm.t[:, :], in_=gsm.t[:, :]), R=[gsm.b], W=[gsm.b])
            kb.op("dve", lambda e: e.tensor_tensor(out=gts, in0=gts, in1=bl(gsm.t[:, :], 8, 16), op=ALU.mult),
                  R=[gts_all.b, gsm.b], W=[gts_all.b])
            kb.op("dve", lambda e: e.tensor_scalar(out=PJ.t[:, :, :], in0=PI.t[:, :, :], scalar1=4, scalar2=None,
                                                    op0=ALU.logical_shift_right), R=[PI.b], W=[PJ.b])
            kb.op("dve", lambda e: e.tensor_copy(out=jf.t[:, :, :], in_=PJ.t[:, :, :]), R=[PJ.b], W=[jf.b])
            kb.op("dve", lambda e: e.tensor_copy(out=PIf.t[:, :, :], in_=PI.t[:, :, :]), R=[PI.b], W=[PIf.b])
            kb.op("dve", lambda e: e.scalar_tensor_tensor(out=jpf.t[:, :, :], in0=jf.t[:, :, :], scalar=-16.0, in1=PIf.t[:, :, :],
                                                           op0=ALU.mult, op1=ALU.add), R=[jf.b, PIf.b], W=[jpf.b])
            for h in range(8):
                for (src, hi, dst, dstb) in ((jf, 2 * h, asel, asel_all), (jpf, 2 * h + 1, bsel, bsel_all)):
                    kb.op("dve", lambda e: e.tensor_tensor(out=oh.t[:, :, :], in0=bl(src.t[:, h, :], 16, 16), in1=bm(IOT.t[:, :], 16, 16),
                                                            op=ALU.is_equal), R=[src.b, IOT.b], W=[oh.b])
                    kb.op("dve", lambda e: e.tensor_tensor(out=oh.t[:, :, :], in0=oh.t[:, :, :], in1=bm(I1f.t[:, hi, :], 16, 16),
                                                            op=ALU.mult), R=[oh.b, I1f.b], W=[oh.b])
                    kb.op("dve", lambda e: e.reduce_sum(out=dst[:, h, :], in_=oh.t[:, :, :], axis=mybir.AxisListType.X),
                          R=[oh.b], W=[dstb.b])
            for w_, srcall in enumerate((asel_all, bsel_all, gts_all)):
                kb.op("dve", lambda e: e.tensor_copy(out=trb.t[:, w_, :], in_=srcall.t[:, tt, :]), R=[srcall.b], W=[trb.b])
            kb.tr([(pst[0].t[:, w_ * 128:(w_ + 1) * 128], trb.t[:, w_, :]) for w_ in range(3)], identb.t[:, :],
                  R=[trb.b, identb.b], W=[pst[0].b])
            kb.op("act", lambda e: e.copy(out=abT.t[:, :, :].rearrange("p a b -> p (a b)"), in_=pst[0].t[:, 0:384]),
                  R=[pst[0].b], W=[abT.b])
            for qt in range(4):
                hs_ = slice(qt * 32, (qt + 1) * 32)
                kb.op("dve", lambda e: e.tensor_tensor(out=OHB.t[:, :, :], in0=bl(abT.t[:, 1, hs_], 32, 128), in1=bm(IO128.t[:, :], 32, 128),
                                                        op=ALU.is_equal), R=[abT.b, IO128.b], W=[OHB.b])
                kb.op("dve", lambda e: e.tensor_tensor(out=OHA.t[:, :, :], in0=bl(abT.t[:, 0, hs_], 32, 128), in1=bm(IO128.t[:, :], 32, 128),
                                                        op=ALU.is_equal), R=[abT.b, IO128.b], W=[OHA.b])
                kb.op("dve", lambda e: e.tensor_tensor(out=OHA.t[:, :, :], in0=OHA.t[:, :, :], in1=bl(abT.t[:, 2, hs_], 32, 128),
                                                        op=ALU.mult), R=[OHA.b, abT.b], W=[OHA.b])
                for t4 in range(8):
                    ps = next_ps()
                    for q in range(4):
                        tl = t4 * 4 + q
                        kb.mm(ps.t[:, q * 128:(q + 1) * 128], [(OHB.t[:, tl, :], OHA.t[:, tl, :])], R=[OHB.b, OHA.b], W=[ps.b])
                    for q in range(4):
                        t = qt * 32 + t4 * 4 + q
                        kb.op("act", lambda e: e.copy(out=GS.t[:, :, t], in_=ps.t[:, q * 128:(q + 1) * 128]), R=[ps.b], W=[GS.b])
            kb.dma([(Gd[16 * i:16 * (i + 1), :, tt * 128:(tt + 1) * 128].rearrange("a b t -> b a t"), GS.t[:, 16 * i:16 * (i + 1), :])
                    for i in range(8)], R=[GS.b])
        kb.barrier()
        PS_.close()
        P3 = ExitStack()
        ubs = [kb.sb(P3, f"ub{i}", [128, 2048], BF16) for i in range(2)]
        uTs = [kb.sb(P3, f"uT{i}", [128, 16, 128], BF16) for i in range(2)]
        vb = [kb.sb(P3, f"vb{i}", [128, 2048], BF16) for i in range(16)]
        Ga = [kb.sb(P3, f"Ga{i}", [128, SEGW], BF16) for i in range(2)]
        hgs = [kb.sb(P3, f"hg{i}", [128, SEGW], BF16) for i in range(2)]
        Aa = [kb.sb(P3, f"Aa{i}", [128, SEGW], BF16) for i in range(8)]
        for a in range(128):
            i2, i8 = a % 2, a % 8
            i16 = a % 16
            ub = ubs[i2]
            uT = uTs[i2]
            hg = hgs[i2]
            kb.dma([(ub.t[:, :], pu[a * 128:(a + 1) * 128, :])], W=[ub.b], q="pool")
            for q2 in range(2):
                pt = pst[q2]
                kb.tr([(pt.t[:, k * 128:(k + 1) * 128], ub.t[:, (q2 * 8 + k) * 128:(q2 * 8 + k + 1) * 128]) for k in range(8)],
                      identb.t[:, :], R=[ub.b, identb.b], W=[pt.b])
                kb.op("act", lambda e: e.copy(out=uT.t[:, q2 * 8:(q2 + 1) * 8, :].rearrange("p a b -> p (a b)"), in_=pt.t[:, :]),
                      R=[pt.b], W=[uT.b])
            kb.dma([(vb[i16].t[:, :], pv[a * 128:(a + 1) * 128, :])], W=[vb[i16].b], q="pool")
            kb.dma([(Ga[i2].t[:, 0:ntok], Gd[a, :, 0:ntok])], W=[Ga[i2].b])
            for (t0, n) in tbs:
                ps = next_ps()
                kb.mm(ps.t[:, 0:n], [(uT.t[:, c, :], xnT.t[:, c, t0:t0 + n]) for c in range(16)], R=[uT.b, xnT.b], W=[ps.b])
                kb.op("act", lambda e: e.activation(out=hg.t[:, t0:t0 + n], in_=ps.t[:, 0:n], func=AF.Gelu), R=[ps.b], W=[hg.b])
                kb.op("dve", lambda e: e.tensor_tensor(out=Aa[i8].t[:, t0:t0 + n], in0=hg.t[:, t0:t0 + n], in1=Ga[i2].t[:, t0:t0 + n],
                                                        op=ALU.mult), R=[hg.b, Ga[i2].b], W=[Aa[i8].b])
            if i8 == 7:
                for dj in range(16):
                    for (t0, n) in tbs:
                        ps = next_ps()
                        vo = (a // 8 % 2) * 8
                        kb.mm(ps.t[:, 0:n], [(vb[vo + s_].t[:, dj * 128:(dj + 1) * 128], Aa[s_].t[:, t0:t0 + n]) for s_ in range(8)],
                              R=[x.b for x in vb[vo:vo + 8]] + [x.b for x in Aa], W=[ps.b])
                        kb.op("dve", lambda e: e.tensor_tensor(out=XT.t[:, dj, t0:t0 + n], in0=XT.t[:, dj, t0:t0 + n], in1=ps.t[:, 0:n],
                                                                op=ALU.add), R=[XT.b, ps.b], W=[XT.b])
        kb.barrier()
        P3.close()
        P0.close()

    segs = [("pre", xpT, 0, 512, False), ("pre", xpT, 512, 512, False),
            ("main", xT, 0, 512, False), ("main", xT, 512, 640, True)]
    for si, (kind, src, c0, ntok, has_s) in enumerate(segs):
        full = kind == "main"
        MX = ExitStack()
        if has_s:
            histx = kb.sb(MX, "histx", [128, 24, 48])
            hists = kb.sb(MX, "hists", [128, 16, 32])
            kb.dma([(histx.t[:, :, :], stconvT.rearrange("t p f -> p t f")), (hists.t[:, :, :], stscT.rearrange("t p f -> p t f"))],
                   W=[histx.b, hists.b], owner=histx.b)
            oconv_s = kb.sb(MX, "oconv_s", [128, 24, 16, 3])
            osc_s = kb.sb(MX, "osc_s", [128, 16, 16, 2])
        xdt = kb.sb(MX, "xdt", [128, 512], BF16)
        xdtd = kb.sb(MX, "xdtd", [128, 512], BF16)
        Btok = kb.sb(MX, "Btok", [128, 128], BF16)
        xstok = kb.sb(MX, "xstok", [128, 512], BF16)
        sztok = kb.sb(MX, "sztok", [128, 512], BF16)
        sm = kb.sb(MX, "sm", [128, 64])
        T1 = kb.sb(MX, "T1", [128, 8, 128])
        rhs_bc = T1
        Eb = kb.sb(MX, "Eb", [128, 8, 128], BF16)
        Mb = kb.sb(MX, "Mb", [128, 8, 128], BF16)
        y1 = kb.sb(MX, "y1", [128, 512])
        ynb = kb.sb(MX, "ynb", [128, 512], BF16)
        STb = kb.sb(MX, "STb", [128, 512], BF16)
        yTblk = kb.sb(MX, "yTblk", [128, 4, SEGW], BF16)
        hb = [kb.sb(MX, f"hb{i}", [128, 3 + 512]) for i in range(2)]
        acc = [kb.sb(MX, f"acc{i}", [128, 512]) for i in range(2)]
        y2 = acc[0]
        hs = kb.sb(MX, "hs", [128, 16, 11])
        cvT = kb.sb(MX, "cvT", [128, 6, SEGW], BF16)
        szT = kb.sb(MX, "szT", [128, 4, SEGW], BF16)
        CTz = kb.sb(MX, "CTz", [128, 16, 128], BF16)
        stf = [kb.sb(MX, f"stf{i}", [128, 512]) for i in range(2)]
        stb = [kb.sb(MX, f"stb{i}", [128, 512], BF16) for i in range(2)]
        Bz = [kb.sb(MX, f"Bz{i}", [128, 128], BF16) for i in range(2)]
        so = [kb.sb(MX, f"so{i}", [128, 512]) for i in range(2)]
        rhs_s = kb.sb(MX, "rhs_s", [128, 16, 8])
        dch_s = kb.sb(MX, "dch_s", [128, 16, 8])
        dtt = kb.sb(MX, "dtt", [128, 5, 32])
        at = kb.sb(MX, "at", [128, 5, 32])
        hsb = kb.sb(MX, "hsb", [128, 512])
        pb = [kb.sb(MX, f"pb{i}", [128, 2 + 512]) for i in range(2)]
        pbs = kb.sb(MX, "pbs", [128, 16, 10])
        hl = kb.sb(MX, "hl", [128, 2])
        cbS = kb.sb(MX, "cbS", [128, 128])

        kb.dma([(XT.t[:, 4 * i:4 * i + 4, 0:ntok], src[512 * i:512 * (i + 1), c0:c0 + ntok].rearrange("(c p) t -> p c t", p=128))
                for i in range(4)], W=[XT.b])
        if KSTOP == 0:
            break
        S1 = ExitStack()
        rmsnormT(XT, ntok, PV_GMIX, xnT, S1)
        kb.barrier()
        S1.close()
        if KSTOP == 1:
            break
        if si == 1:
            kb.op("dve", lambda e: e.tensor_copy(out=xnl.t[:, :, :], in_=xnT.t[:, :, 510:512]), R=[xnT.b], W=[xnl.b])
        dt_pass(5 if has_s else 4)
        if KSTOP == 2:
            break
        ssd_seg(full, has_s, si == 3)
        if KSTOP == 3 + si:
            break
        if si == 1:
            for g in range(4):
                kb.op("dve", lambda e: e.tensor_scalar(out=STall.t[:, g, :], in0=STall.t[:, g, :], scalar1=FL.t[:, 0:1],
                                                        scalar2=None, op0=ALU.mult), R=[STall.b, FL.b], W=[STall.b])
        if full:
            sc_seg(si == 2, has_s)
        if has_s:
            kb.dma([(o_sconv[:, :], oconv_s.t[:, :, :, :].rearrange("p a b c -> p (a b c)"))], R=[oconv_s.b])
            kb.dma([(o_ssc[:, :], osc_s.t[:, :, :, :].rearrange("p a b c -> p (a b c)"))], R=[osc_s.b])
        kb.barrier()
        MX.close()
        if full:
            if STAGE >= 2:
                attention_seg(si == 2, has_s, ntok)
            if STAGE >= 3:
                peer_seg(si == 2, has_s, ntok)
            S4 = ExitStack()
            rmsnormT(XT, ntok, PV_GFIN, None, S4, out_f32_dram=yT[:, c0:c0 + ntok])
            kb.barrier()
            S4.close()
    kb.dma([(o_pssm[:, :], STall.t[:, :, :].rearrange("p g f -> p (g f)"))], R=[STall.b])
    kb.dma([(o_pconv[:, :], halo_x.t[:, :, :].rearrange("p a b -> p (a b)"))], R=[halo_x.b])
    kb.dma([(o_psc[:, :], halo_sc.t[:, :, :].rearrange("p a b -> p (a b)"))], R=[halo_sc.b])
    kb.finish()
    A.close()
    G.close()
    return nc


def tile_w(W):
    K, N = W.shape
    return np.ascontiguousarray(W.reshape(K // 128, 128, N // 128, 128).transpose(2, 1, 0, 3))


_CACHE = {}


def make_cst():
    c = np.zeros((128, NCST), np.float32)
    s = np.arange(128)[:, None]
    l = np.arange(128)[None, :]
    c[:, C_ID:C_ID + 128] = (s == l)
    c[:, C_TRP:C_TRP + 128] = (s <= l)
    c[:, C_NGP:C_NGP + 128] = np.where(l >= s, 0.0, -30000.0)
    same = (s // 8) == (l // 8)
    c[:, C_TRS:C_TRS + 128] = (s <= l) & same
    c[:, C_NGS:C_NGS + 128] = np.where((l >= s) & same, 0.0, -30000.0)
    c[:, C_TOTS:C_TOTS + 128] = same
    c[:, C_RM:C_RM + 16] = (s // 8) == np.arange(16)[None, :]
    sel = (np.arange(128)[None, :] // 8) == np.arange(16)[:, None]
    c[:, C_SEL:C_SEL + 2048] = sel.reshape(1, 2048).astype(np.float32)
    c[:, C_IOTA:C_IOTA + 16] = np.arange(16)[None, :]
    c[:, C_IO128:C_IO128 + 128] = np.arange(128)[None, :]
    return c


def kernel(**inp):
    f = lambda k: np.asarray(inp[k], dtype=np.float32)
    x_prompt, x_sample = f("x_prompt"), f("x_sample")
    if "nc" not in _CACHE:
        _CACHE["nc"] = build()
    nc = _CACHE["nc"]
    w_in = f("w_in")[0]
    shared = {}
    shared["wz"] = tile_w(w_in[:, 0:2048])
    shared["wxbc"] = tile_w(w_in[:, 2048:5120])
    shared["wdt"] = np.ascontiguousarray(w_in[:, 5120:5152].reshape(16, 128, 32).transpose(1, 0, 2))
    shared["wsch"] = tile_w(w_in[:, 5152:7200])
    shared["wscb"] = tile_w(w_in[:, 7200:9248])
    shared["wscc"] = tile_w(w_in[:, 9248:11296])
    wo = f("w_out")[0]
    shared["wout"] = np.ascontiguousarray(wo.reshape(8, 4, 128, 16, 128).transpose(0, 3, 2, 1, 4))
    pv = np.zeros((128, NPV), np.float32)
    col = lambda v: v.reshape(-1, 128).T
    pv[:, PV_GMIX:PV_GMIX + 16] = col(f("norm_mix")[0])
    pv[:, PV_GMQ:PV_GMQ + 16] = col(f("norm_mem_q")[0])
    pv[:, PV_GMKV:PV_GMKV + 16] = col(f("norm_mem_kv")[0])
    pv[:, PV_GFFN:PV_GFFN + 16] = col(f("norm_ffn")[0])
    pv[:, PV_GFIN:PV_GFIN + 16] = col(f("norm_final"))
    pv[:, PV_SSDN:PV_SSDN + 16] = col(f("ssd_norm")[0])
    cw = f("ssd_conv_w")[0]
    pv[:, PV_CW:PV_CW + 96] = cw.reshape(4, 24, 128).transpose(2, 1, 0).reshape(128, 96)
    pv[:, PV_CB:PV_CB + 24] = col(f("ssd_conv_b")[0])
    scw = f("sc_conv_w")[0]
    pv[:, PV_SCW:PV_SCW + 48] = scw.reshape(3, 16, 128).transpose(2, 1, 0).reshape(128, 48)
    shared["pvec"] = pv
    tk = np.zeros((128, 96), np.float32)
    tk[:, 0:32] = f("ssd_dt_bias")[0][None, :]
    tk[:, 32:64] = f("ssd_a_log")[0][None, :]
    tk[:, 64:96] = f("ssd_d")[0][None, :]
    shared["tokp"] = tk
    shared["cst"] = make_cst()
    if STAGE >= 2:
        for nm, key in (("wmq", "w_mem_q"), ("wmk", "w_mem_k"), ("wmv", "w_mem_v"), ("wmo", "w_mem_o")):
            shared[nm] = tile_w(f(key)[0])
    if STAGE >= 3:
        shared["wpq"] = tile_w(f("w_peer_q")[0])
        sk = f("peer_sub_keys")[0]
        shared["skT"] = np.ascontiguousarray(sk.reshape(16, 128, 128).transpose(0, 2, 1))
        shared["pu"] = f("peer_u")[0]
        shared["pv"] = f("peer_v")[0]
    st_ssm, st_conv, st_sc = f("state_ssm")[0], f("state_ssd_conv")[0], f("state_short_conv")[0]
    in_maps = []
    for c in range(8):
        b, half = c // 2, c % 2
        sq = slice(16 * c, 16 * c + 16)
        own = x_prompt[b, half * 1024:(half + 1) * 1024]
        xs = x_sample[sq].reshape(128, 2048)
        m = dict(shared)
        m["xT"] = np.ascontiguousarray(np.concatenate([own, xs], 0).T)
        m["xpT"] = np.ascontiguousarray(x_prompt[b, 0:1024].T) if half == 1 else np.zeros((2048, 1024), np.float32)
        m["flag"] = np.full((128, 1), float(half), np.float32)
        m["stT"] = np.ascontiguousarray(st_ssm[sq].transpose(0, 3, 1, 2).reshape(16, 128, 2048))
        m["stconvT"] = np.ascontiguousarray(st_conv[sq].transpose(2, 0, 1).reshape(24, 128, 48))
        m["stscT"] = np.ascontiguousarray(st_sc[sq].transpose(2, 0, 1).reshape(16, 128, 32))
        if STAGE >= 2:
            m["memT"] = np.ascontiguousarray(f("mem_prompt")[b].T)
            m["ckT"] = np.ascontiguousarray(f("cache_mem_k")[0][sq].reshape(16, 256, 2048).transpose(0, 2, 1))
            m["cv"] = np.ascontiguousarray(f("cache_mem_v")[0][sq].reshape(16, 256, 2048))
        in_maps.append(m)
    res = run_bass_kernel_spmd(nc, in_maps, core_ids=list(range(8))).results
    y_p = np.zeros((4, 2048, 2048), np.float32)
    y_s = np.zeros((128, 8, 2048), np.float32)
    p_ssm = np.zeros((1, 4, 32, 64, 128), np.float32)
    p_conv = np.zeros((1, 4, 3, 3072), np.float32)
    p_sc = np.zeros((1, 4, 2, 2048), np.float32)
    p_mk = np.zeros((1, 4, 256, 4, 512), np.float32)
    p_mv = np.zeros((1, 4, 256, 4, 512), np.float32)
    s_ssm = np.zeros((1, 128, 32, 64, 128), np.float32)
    s_conv = np.zeros((1, 128, 3, 3072), np.float32)
    s_sc = np.zeros((1, 128, 2, 2048), np.float32)
    for c in range(8):
        r = res[c]
        b, half = c // 2, c % 2
        sq = slice(16 * c, 16 * c + 16)
        y = r["yT"].T
        y_p[b, half * 1024:(half + 1) * 1024] = y[0:1024]
        y_s[sq] = y[1024:1152].reshape(16, 8, 2048)
        s_ssm[0, sq] = r["o_sssm"].reshape(16, 128, 32, 64).transpose(0, 2, 3, 1)
        s_conv[0, sq] = r["o_sconv"].reshape(128, 24, 16, 3).transpose(2, 3, 1, 0).reshape(16, 3, 3072)
        s_sc[0, sq] = r["o_ssc"].reshape(128, 16, 16, 2).transpose(2, 3, 1, 0).reshape(16, 2, 2048)
        if half == 1:
            p_ssm[0, b] = r["o_pssm"].reshape(128, 32, 64).transpose(1, 2, 0)
            p_conv[0, b] = r["o_pconv"].reshape(128, 24, 3).transpose(2, 1, 0).reshape(3, 3072)
            p_sc[0, b] = r["o_psc"].reshape(128, 16, 2).transpose(2, 1, 0).reshape(2, 2048)
        if STAGE >= 2 and half == 0:
            p_mk[0, b] = r["o_mk"].T.reshape(256, 4, 512)
            p_mv[0, b] = r["o_mv"].T.reshape(256, 4, 512)
    return (y_p, y_s, p_ssm, p_conv, p_sc, p_mk, p_mv, s_ssm, s_conv, s_sc)
```

```python
import math
import numpy as np
from contextlib import ExitStack
import concourse.bass as bass
import concourse.mybir as mybir
from concourse.bass_utils import run_bass_kernel_spmd

F32 = mybir.dt.float32
BF16 = mybir.dt.bfloat16
I32 = mybir.dt.int32
U32 = mybir.dt.uint32
AF = mybir.ActivationFunctionType
ALU = mybir.AluOpType

NT, NPR, NS = 1152, 1024, 128
SEGW = 640
EPS = 1e-6
STAGE = 3
KSTOP = 99
KSUB = 99
KDBG = 0


class Ev:
    __slots__ = ("sem", "val")

    def __init__(self, sem, val):
        self.sem, self.val = sem, val


class Buf:
    def __init__(self, name):
        self.name = name
        self.lw = None
        self.rd = {}
        self.dsem = None
        self.dval = 0


class Tile:
    def __init__(self, t, b):
        self.t, self.b = t, b


class KB:
    def __init__(self):
        self.nc = bass.Bass("TRN2", target_bir_lowering=False)
        self.es = ExitStack()
        nc = self.nc
        self.eng = {"pe": nc.tensor, "dve": nc.vector, "act": nc.scalar, "pool": nc.gpsimd, "sp": nc.sync}
        self.esem = {k: self.es.enter_context(nc.semaphore("s_" + k)) for k in ("pe", "dve", "act", "pool")}
        self.ecnt = {k: 0 for k in self.esem}
        self.waited = {k: {} for k in self.eng}
        self.bufs = []
        self.dram = {}
        self.n = 0
        self.named = {}

    def din(self, name, shape, dt=F32):
        self.dram[name] = self.nc.dram_tensor(name, list(shape), dt, kind="ExternalInput").ap()
        return self.dram[name]

    def dout(self, name, shape, dt=F32):
        self.dram[name] = self.nc.dram_tensor(name, list(shape), dt, kind="ExternalOutput").ap()
        return self.dram[name]

    def buf(self, name):
        b = Buf(name)
        self.bufs.append(b)
        return b

    def sb(self, stack, name, shape, dt=F32):
        self.n += 1
        t = stack.enter_context(self.nc.sbuf_tensor(f"{name}_{self.n}", list(shape), dt))
        return Tile(t, self.buf(name))

    def psum(self, stack, name, shape, dt=F32):
        t = stack.enter_context(self.nc.psum_tensor(name, list(shape), dt))
        return Tile(t, self.buf(name))

    def _wait(self, e, ev):
        if ev is None:
            return
        if e == "pe" and ev.sem is self.esem["pe"]:
            return
        w = self.waited[e]
        k = id(ev.sem)
        if w.get(k, 0) >= ev.val:
            return
        self.eng[e].wait_ge(ev.sem, ev.val)
        w[k] = ev.val

    def _deps(self, e, R, W):
        for b in R:
            self._wait(e, b.lw)
        for b in W:
            self._wait(e, b.lw)
            for r in b.rd.values():
                self._wait(e, r)

    def _commit(self, ev, R, W):
        for b in R:
            k = id(ev.sem)
            o = b.rd.get(k)
            if o is None or o.val < ev.val:
                b.rd[k] = ev
        for b in W:
            b.lw = ev
            b.rd = {}

    def op(self, e, fn, R=(), W=()):
        self._deps(e, R, W)
        ins = fn(self.eng[e])
        self.ecnt[e] += 1
        ev = Ev(self.esem[e], self.ecnt[e])
        ins.then_inc(ev.sem, 1)
        self._commit(ev, R, W)
        return ev

    def mm(self, out, pairs, R=(), W=(), start=True, stop=True):
        self._deps("pe", R, W)
        n = len(pairs)
        ins = None
        for i, (l, r) in enumerate(pairs):
            ins = self.nc.tensor.matmul(out, lhsT=l, rhs=r, start=(start and i == 0), stop=(stop and i == n - 1))
        self.ecnt["pe"] += 1
        ev = Ev(self.esem["pe"], self.ecnt["pe"])
        ins.then_inc(ev.sem, 1)
        self._commit(ev, R, W)
        return ev

    def tr(self, items, ident, R=(), W=()):
        self._deps("pe", R, W)
        ins = None
        for o, i in items:
            ins = self.nc.tensor.transpose(out=o, in_=i, identity=ident)
        self.ecnt["pe"] += 1
        ev = Ev(self.esem["pe"], self.ecnt["pe"])
        ins.then_inc(ev.sem, 1)
        self._commit(ev, R, W)
        return ev

    def dma(self, pairs, R=(), W=(), q="sp", owner=None):
        self._deps(q, R, W)
        owner = owner or (W[0] if W else R[0])
        if owner.dsem is None:
            if owner.name in self.named:
                owner.dsem, owner.dval = self.named[owner.name]
            else:
                owner.dsem = self.es.enter_context(self.nc.semaphore("d%d_%s" % (len(self.bufs), owner.name)))
        for o, i in pairs:
            self.eng[q].dma_start(out=o, in_=i).then_inc(owner.dsem, 16)
            owner.dval += 16
        self.named[owner.name] = (owner.dsem, owner.dval)
        ev = Ev(owner.dsem, owner.dval)
        self._commit(ev, R, W)
        return ev

    def gather(self, out, table, idx_ap, R=(), W=()):
        q = "pool"
        self._deps(q, R, W)
        owner = W[0]
        if owner.dsem is None:
            owner.dsem = self.es.enter_context(self.nc.semaphore("g%d_%s" % (len(self.bufs), owner.name)))
        self.nc.gpsimd.indirect_dma_start(
            out=out, out_offset=None, in_=table, in_offset=bass.IndirectOffsetOnAxis(ap=idx_ap, axis=0)
        ).then_inc(owner.dsem, 16)
        owner.dval += 16
        ev = Ev(owner.dsem, owner.dval)
        self._commit(ev, R, W)
        return ev

    def barrier(self):
        evs = [Ev(self.esem[k], self.ecnt[k]) for k in self.esem if self.ecnt[k] > 0]
        evs += [Ev(b.dsem, b.dval) for b in self.bufs if b.dsem is not None and b.dval > 0]
        for e in self.eng:
            for ev in evs:
                if e in self.esem and ev.sem is self.esem[e]:
                    pass
                w = self.waited[e]
                k = id(ev.sem)
                if w.get(k, 0) >= ev.val:
                    continue
                self.eng[e].wait_ge(ev.sem, ev.val)
                w[k] = ev.val

    def finish(self):
        for b in self.bufs:
            if b.dsem is not None and b.dval > 0:
                self._wait("sp", Ev(b.dsem, b.dval))
        for k in self.esem:
            if self.ecnt[k] > 0:
                self._wait("sp", Ev(self.esem[k], self.ecnt[k]))


def bl(ap, k, n):
    return ap.unsqueeze(2).to_broadcast([128, k, n])


def bm(ap, k, n):
    return ap.unsqueeze(1).to_broadcast([128, k, n])


PV_GMIX, PV_GMQ, PV_GMKV, PV_GFFN, PV_GFIN, PV_SSDN = 0, 16, 32, 48, 64, 80
PV_CW, PV_CB, PV_SCW = 96, 192, 216
NPV = 264
C_ID, C_TRP, C_NGP, C_TRS, C_NGS, C_TOTS, C_RM, C_SEL = 0, 128, 256, 384, 512, 640, 768, 784
C_IOTA = 784 + 2048
C_IO128 = 784 + 2048 + 16
NCST = 784 + 2048 + 16 + 128


def build():
    kb = KB()
    nc = kb.nc
    G = ExitStack()
    xT = kb.din("xT", [2048, NT])
    xpT = kb.din("xpT", [2048, NPR])
    flag = kb.din("flag", [128, 1])
    pvec = kb.din("pvec", [128, NPV])
    tokp = kb.din("tokp", [128, 96])
    cst = kb.din("cst", [128, NCST])
    wz = kb.din("wz", [16, 128, 16, 128])
    wxbc = kb.din("wxbc", [24, 128, 16, 128])
    wdt = kb.din("wdt", [128, 16, 32])
    wsch = kb.din("wsch", [16, 128, 16, 128])
    wscb = kb.din("wscb", [16, 128, 16, 128])
    wscc = kb.din("wscc", [16, 128, 16, 128])
    wout = kb.din("wout", [8, 16, 128, 4, 128])
    stT = kb.din("stT", [16, 128, 2048])
    stconvT = kb.din("stconvT", [24, 128, 48])
    stscT = kb.din("stscT", [16, 128, 32])
    yT = kb.dout("yT", [2048, NT])
    o_pssm = kb.dout("o_pssm", [128, 2048])
    o_pconv = kb.dout("o_pconv", [128, 24 * 3])
    o_psc = kb.dout("o_psc", [128, 16 * 2])
    o_sssm = kb.dout("o_sssm", [16, 128, 2048])
    o_sconv = kb.dout("o_sconv", [128, 24 * 48])
    o_ssc = kb.dout("o_ssc", [128, 16 * 32])
    if KDBG:
        dbg1 = kb.dout("dbg1", [128, 512])
        dbg2 = kb.dout("dbg2", [128, 512])
    if STAGE >= 2:
        memT = kb.din("memT", [2048, 256])
        wmq = kb.din("wmq", [16, 128, 16, 128])
        wmk = kb.din("wmk", [16, 128, 16, 128])
        wmv = kb.din("wmv", [16, 128, 16, 128])
        wmo = kb.din("wmo", [16, 128, 16, 128])
        ckT = kb.din("ckT", [16, 2048, 256])
        cv = kb.din("cv", [16, 256, 2048])
        o_mk = kb.dout("o_mk", [2048, 256])
        o_mv = kb.dout("o_mv", [2048, 256])
    if STAGE >= 3:
        wpq = kb.din("wpq", [16, 128, 16, 128])
        skT = kb.din("skT", [16, 128, 128])
        pu = kb.din("pu", [16384, 2048])
        pv = kb.din("pv", [16384, 2048])
        Gd = nc.dram_tensor("Gd", [128, 128, SEGW], BF16, kind="Internal").ap()
    if STAGE >= 2:
        KVd = nc.dram_tensor("KVd", [128, 8192], BF16, kind="Internal").ap()

    XT = kb.sb(G, "XT", [128, 16, SEGW])
    PV = kb.sb(G, "PV", [128, NPV])
    TK = kb.sb(G, "TK", [128, 96])
    identb = kb.sb(G, "identb", [128, 128], BF16)
    onesm = kb.sb(G, "onesm", [128, 128], BF16)
    onesb = kb.sb(G, "onesb", [128, 128], BF16)
    NW = 4
    wb = [kb.sb(G, f"wb{i}", [128, 16, 128], BF16) for i in range(NW)]
    rstd_bc = kb.sb(G, "rstd_bc", [128, 512])
    pst = [kb.psum(G, f"pst{i}", [128, 1024], BF16) for i in range(2)]
    psW = kb.psum(G, "psW", [128, 1024], F32)
    pg = [kb.psum(G, f"pg{i}", [128, 512], F32) for i in range(4)]
    st = {"w": 0, "p": 0}

    def next_ps():
        st["p"] = (st["p"] + 1) % 3
        return pg[st["p"]]

    psE = pg[3]

    def load_w(src, kc=16):
        i = st["w"] % NW
        st["w"] += 1
        kb.dma([(wb[i].t[:, 0:kc, :], src)], W=[wb[i].b], q="pool")
        return wb[i]

    TBF = [(0, 512), (512, 128)]
    TBP = [(0, 512)]

    def linearT(blocks, srcT, tbs, evac, kc=16):
        for j, src in enumerate(blocks):
            slot = load_w(src, kc)
            for bi, (t0, n) in enumerate(tbs):
                ps = next_ps()
                kb.mm(ps.t[:, 0:n], [(slot.t[:, c, :], srcT.t[:, c, t0:t0 + n]) for c in range(kc)],
                      R=[slot.b, srcT.b], W=[ps.b])
                evac(j, bi, t0, n, ps)

    def rmsnormT(src, ntok, gcol, dst, S, out_f32_dram=None):
        sq = kb.sb(S, "sq", [128, 16, 256], BF16)
        lnv = kb.sb(S, "lnv", [128, 512])
        ost = [kb.sb(S, f"ost{i}", [128, 512]) for i in range(2)] if out_f32_dram is not None else None
        for t0 in range(0, ntok, 256):
            n = min(256, ntok - t0)
            kb.op("act", lambda e: e.activation(out=sq.t[:, :, 0:n], in_=src.t[:, :, t0:t0 + n], func=AF.Square),
                  R=[src.b], W=[sq.b])
            ps = next_ps()
            kb.mm(ps.t[:, 0:n], [(onesm.t[:, :], sq.t[:, c, 0:n]) for c in range(16)], R=[onesm.b, sq.b], W=[ps.b])
            kb.op("act", lambda e: e.activation(out=lnv.t[:, 0:n], in_=ps.t[:, 0:n], func=AF.Ln, bias=EPS),
                  R=[ps.b], W=[lnv.b])
            kb.op("act", lambda e: e.activation(out=rstd_bc.t[:, 0:n], in_=lnv.t[:, 0:n], func=AF.Exp, scale=-0.5),
                  R=[lnv.b], W=[rstd_bc.b])
            for c in range(16):
                if out_f32_dram is None:
                    kb.op("dve", lambda e: e.scalar_tensor_tensor(
                        out=dst.t[:, c, t0:t0 + n], in0=src.t[:, c, t0:t0 + n], scalar=PV.t[:, gcol + c:gcol + c + 1],
                        in1=rstd_bc.t[:, 0:n], op0=ALU.mult, op1=ALU.mult), R=[src.b, PV.b, rstd_bc.b], W=[dst.b])
                else:
                    o = ost[c % 2]
                    kb.op("dve", lambda e: e.scalar_tensor_tensor(
                        out=o.t[:, 0:n], in0=src.t[:, c, t0:t0 + n], scalar=PV.t[:, gcol + c:gcol + c + 1],
                        in1=rstd_bc.t[:, 0:n], op0=ALU.mult, op1=ALU.mult), R=[src.b, PV.b, rstd_bc.b], W=[o.b])
                    kb.dma([(out_f32_dram[c * 128:(c + 1) * 128, t0:t0 + n], o.t[:, 0:n])], R=[o.b])

    kb.dma([(PV.t[:, :], pvec[:, :]), (TK.t[:, :], tokp[:, :])], W=[PV.b, TK.b], owner=PV.b)
    A = ExitStack()
    CST = kb.sb(A, "CST", [128, 784])
    SEL = kb.sb(A, "SEL", [128, 16, 128], BF16)
    T0 = ExitStack()
    SELF = kb.sb(T0, "SELF", [128, 2048])
    kb.dma([(CST.t[:, :], cst[:, 0:784]), (SELF.t[:, :], cst[:, 784:784 + 2048])], W=[CST.b, SELF.b], owner=CST.b)
    kb.op("pool", lambda e: e.tensor_copy(out=SEL.t[:, :, :], in_=SELF.t[:, :].rearrange("p (b l) -> p b l", b=16)),
          R=[SELF.b], W=[SEL.b])
    kb.barrier()
    T0.close()
    kb.op("dve", lambda e: e.tensor_copy(out=identb.t[:, :], in_=CST.t[:, C_ID:C_ID + 128]), R=[CST.b], W=[identb.b])
    kb.op("dve", lambda e: e.memset(onesm.t[:, :], 1.0 / 2048.0), W=[onesm.b])
    kb.op("dve", lambda e: e.memset(onesb.t[:, :], 1.0), W=[onesb.b])
    onesf = kb.sb(A, "onesf", [128, 128])
    kb.op("dve", lambda e: e.memset(onesf.t[:, :], 1.0), W=[onesf.b])
    FL = kb.sb(A, "FL", [128, 1])
    kb.dma([(FL.t[:, :], flag[:, :])], W=[FL.b])
    Abc = kb.sb(A, "Abc", [128, 32])
    kb.op("act", lambda e: e.activation(out=Abc.t[:, :], in_=TK.t[:, 32:64], func=AF.Exp), R=[TK.b], W=[Abc.b])
    kb.op("dve", lambda e: e.tensor_scalar(out=Abc.t[:, :], in0=Abc.t[:, :], scalar1=-1.0, scalar2=None, op0=ALU.mult),
          R=[Abc.b], W=[Abc.b])
    wdtb = kb.sb(A, "wdtb", [128, 16, 32], BF16)
    T0 = ExitStack()
    wdtf = kb.sb(T0, "wdtf", [128, 16, 32])
    kb.dma([(wdtf.t[:, :, :], wdt[:, :, :])], W=[wdtf.b])
    kb.op("dve", lambda e: e.tensor_copy(out=wdtb.t[:, :, :], in_=wdtf.t[:, :, :]), R=[wdtf.b], W=[wdtb.b])
    kb.barrier()
    T0.close()

    STall = kb.sb(A, "STall", [128, 4, 512])
    halo_x = kb.sb(A, "halo_x", [128, 24, 3])
    halo_sc = kb.sb(A, "halo_sc", [128, 16, 2])
    xnl = kb.sb(A, "xnl", [128, 16, 2], BF16)
    kb.op("dve", lambda e: e.memset(STall.t[:, :, :], 0.0), W=[STall.b])
    kb.op("dve", lambda e: e.memset(halo_x.t[:, :, :], 0.0), W=[halo_x.b])
    IOT = kb.sb(A, "IOT", [128, 16])
    kb.dma([(IOT.t[:, :], cst[:, C_IOTA:C_IOTA + 16])], W=[IOT.b])
    xnT = kb.sb(A, "xnT", [128, 16, SEGW], BF16)
    cnt = {"h": 0}

    def dt_pass(ntile):
        for i in range(ntile):
            ps = next_ps()
            kb.mm(ps.t[:, 0:32], [(xnT.t[:, c, i * 128:(i + 1) * 128], wdtb.t[:, c, :]) for c in range(16)],
                  R=[xnT.b, wdtb.b], W=[ps.b])
            kb.op("dve", lambda e: e.tensor_tensor(out=dtt.t[:, i, :], in0=ps.t[:, 0:32], in1=TK.t[:, 0:32], op=ALU.add),
                  R=[ps.b, TK.b], W=[dtt.b])
            kb.op("act", lambda e: e.activation(out=dtt.t[:, i, :], in_=dtt.t[:, i, :], func=AF.Exp), R=[dtt.b], W=[dtt.b])
            kb.op("act", lambda e: e.activation(out=dtt.t[:, i, :], in_=dtt.t[:, i, :], func=AF.Ln, bias=1.0), R=[dtt.b], W=[dtt.b])
            kb.op("dve", lambda e: e.tensor_tensor(out=at.t[:, i, :], in0=dtt.t[:, i, :], in1=Abc.t[:, :], op=ALU.mult),
                  R=[dtt.b, Abc.b], W=[at.b])

    def wout_apply(kblk, tbs):
        def ev(j, bi, t0, n, ps):
            kb.op("dve", lambda e: e.tensor_tensor(out=XT.t[:, j, t0:t0 + n], in0=XT.t[:, j, t0:t0 + n], in1=ps.t[:, 0:n],
                                                    op=ALU.add), R=[XT.b, ps.b], W=[XT.b])
        linearT([wout[kblk, j] for j in range(16)], yTblk, tbs, ev, kc=4)

    def ssd_seg(full, samp_seg, last_seg):
        tbs = TBF if samp_seg else TBP
        for g in range(4):
            g8 = g * 8
            xbc_tiles = [g * 4 + k for k in range(4)] + [16 + g, 20 + g]
            ST = STall.t[:, g, :]

            def evac_x(j, bi, t0, n, ps, xbc_tiles=xbc_tiles):
                tl = xbc_tiles[j]
                cnt["h"] += 1
                a_ = acc[cnt["h"] % 2]
                if t0 < 512:
                    h = hb[cnt["h"] % 2]
                    kb.op("act", lambda e: e.copy(out=h.t[:, 3:3 + n], in_=ps.t[:, 0:n]), R=[ps.b], W=[h.b])
                    kb.op("dve", lambda e: e.tensor_copy(out=h.t[:, 0:3], in_=halo_x.t[:, tl, :]), R=[halo_x.b], W=[h.b])
                    kb.op("dve", lambda e: e.tensor_copy(out=halo_x.t[:, tl, :], in_=h.t[:, 512:515]), R=[h.b], W=[halo_x.b])
                    kb.op("dve", lambda e: e.tensor_scalar(out=a_.t[:, 0:n], in0=h.t[:, 0:n],
                                                            scalar1=PV.t[:, PV_CW + tl * 4:PV_CW + tl * 4 + 1],
                                                            scalar2=PV.t[:, PV_CB + tl:PV_CB + tl + 1], op0=ALU.mult, op1=ALU.add),
                          R=[h.b, PV.b], W=[a_.b])
                    for q in range(1, 4):
                        kb.op("dve", lambda e: e.scalar_tensor_tensor(
                            out=a_.t[:, 0:n], in0=h.t[:, q:q + n], scalar=PV.t[:, PV_CW + tl * 4 + q:PV_CW + tl * 4 + q + 1],
                            in1=a_.t[:, 0:n], op0=ALU.mult, op1=ALU.add), R=[h.b, PV.b, a_.b], W=[a_.b])
                    kb.op("act", lambda e: e.activation(out=cvT.t[:, j, t0:t0 + n], in_=a_.t[:, 0:n], func=AF.Silu),
                          R=[a_.b], W=[cvT.b])
                else:
                    kb.op("act", lambda e: e.copy(out=a_.t[:, 0:128], in_=ps.t[:, 0:128]), R=[ps.b], W=[a_.b])
                    kb.op("dve", lambda e: e.tensor_copy(out=hs.t[:, :, 3:11], in_=a_.t[:, 0:128].rearrange("p (b t) -> p b t", b=16)),
                          R=[a_.b], W=[hs.b])
                    kb.op("dve", lambda e: e.tensor_copy(out=hs.t[:, :, 0:3],
                                                          in_=histx.t[:, tl, :].rearrange("p (b t) -> p b t", b=16)),
                          R=[histx.b], W=[hs.b])
                    kb.op("dve", lambda e: e.tensor_copy(out=oconv_s.t[:, tl, :, :], in_=hs.t[:, :, 8:11]), R=[hs.b], W=[oconv_s.b])
                    av = a_.t[:, 0:128].rearrange("p (b t) -> p b t", b=16)
                    kb.op("dve", lambda e: e.tensor_scalar(out=av, in0=hs.t[:, :, 0:8],
                                                            scalar1=PV.t[:, PV_CW + tl * 4:PV_CW + tl * 4 + 1],
                                                            scalar2=PV.t[:, PV_CB + tl:PV_CB + tl + 1], op0=ALU.mult, op1=ALU.add),
                          R=[hs.b, PV.b], W=[a_.b])
                    for q in range(1, 4):
                        kb.op("dve", lambda e: e.scalar_tensor_tensor(
                            out=av, in0=hs.t[:, :, q:q + 8], scalar=PV.t[:, PV_CW + tl * 4 + q:PV_CW + tl * 4 + q + 1],
                            in1=av, op0=ALU.mult, op1=ALU.add), R=[hs.b, PV.b, a_.b], W=[a_.b])
                    kb.op("act", lambda e: e.activation(out=cvT.t[:, j, 512:640], in_=a_.t[:, 0:128], func=AF.Silu),
                          R=[a_.b], W=[cvT.b])

            linearT([wxbc[t] for t in xbc_tiles], xnT, tbs, evac_x)
            if KSUB == 1:
                return
            if full:
                def evac_z(j, bi, t0, n, ps):
                    kb.op("act", lambda e: e.activation(out=szT.t[:, j, t0:t0 + n], in_=ps.t[:, 0:n], func=AF.Silu),
                          R=[ps.b], W=[szT.b])
                linearT([wz[g * 4 + k] for k in range(4)], xnT, tbs, evac_z)
            kb.op("act", lambda e: e.copy(out=STb.t[:, :], in_=ST), R=[STall.b], W=[STb.b])

            nchunk = 5 if samp_seg else 4
            for ci in range(nchunk):
                samp = ci == 4
                c0 = ci * 128
                cols = slice(c0, c0 + 128)
                trm = CST.t[:, C_TRS:C_TRS + 128] if samp else CST.t[:, C_TRP:C_TRP + 128]
                ngm = CST.t[:, C_NGS:C_NGS + 128] if samp else CST.t[:, C_NGP:C_NGP + 128]
                totm = CST.t[:, C_TOTS:C_TOTS + 128] if samp else onesf.t[:, :]
                kb.tr([(pst[0].t[:, k * 128:(k + 1) * 128], cvT.t[:, k, cols]) for k in range(5)], identb.t[:, :],
                      R=[cvT.b, identb.b], W=[pst[0].b])
                kb.op("act", lambda e: e.copy(out=xstok.t[:, :], in_=pst[0].t[:, 0:512]), R=[pst[0].b], W=[xstok.b])
                kb.op("dve", lambda e: e.tensor_tensor(out=xdt.t[:, :].rearrange("p (r d) -> p r d", r=8),
                                                        in0=xstok.t[:, :].rearrange("p (r d) -> p r d", r=8),
                                                        in1=bl(dtt.t[:, ci, g8:g8 + 8], 8, 64), op=ALU.mult),
                      R=[xstok.b, dtt.b], W=[xdt.b])
                kb.op("act", lambda e: e.copy(out=Btok.t[:, :], in_=pst[0].t[:, 512:640]), R=[pst[0].b], W=[Btok.b])
                if KSUB == 2:
                    return
                if full:
                    kb.tr([(pst[1].t[:, k * 128:(k + 1) * 128], szT.t[:, k, cols]) for k in range(4)], identb.t[:, :],
                          R=[szT.b, identb.b], W=[pst[1].b])
                    kb.op("act", lambda e: e.copy(out=sztok.t[:, :], in_=pst[1].t[:, 0:512]), R=[pst[1].b], W=[sztok.b])
                psA = next_ps()
                kb.mm(psA.t[:, 0:8], [(trm, at.t[:, ci, g8:g8 + 8])], R=[CST.b, at.b], W=[psA.b])
                kb.mm(psA.t[:, 8:16], [(totm, at.t[:, ci, g8:g8 + 8])], R=[CST.b, onesf.b, at.b], W=[psA.b])
                kb.op("act", lambda e: e.copy(out=sm.t[:, 0:8], in_=psA.t[:, 0:8]), R=[psA.b], W=[sm.b])
                kb.op("dve", lambda e: e.tensor_tensor(out=sm.t[:, 8:16], in0=psA.t[:, 8:16], in1=sm.t[:, 0:8], op=ALU.subtract),
                      R=[psA.b, sm.b], W=[sm.b])
                kb.op("act", lambda e: e.activation(out=sm.t[:, 16:24], in_=sm.t[:, 8:16], func=AF.Exp), R=[sm.b], W=[sm.b])
                kb.op("act", lambda e: e.activation(out=sm.t[:, 24:32], in_=psA.t[:, 8:16], func=AF.Exp), R=[psA.b], W=[sm.b])
                kb.op("dve", lambda e: e.tensor_tensor(out=xdtd.t[:, :].rearrange("p (r d) -> p r d", r=8),
                                                        in0=xdt.t[:, :].rearrange("p (r d) -> p r d", r=8),
                                                        in1=bl(sm.t[:, 16:24], 8, 64), op=ALU.mult),
                      R=[xdt.b, sm.b], W=[xdtd.b])
                if KSUB == 3:
                    return
                if full:
                    kb.op("dve", lambda e: e.tensor_tensor(out=rhs_bc.t[:, :, :], in0=bm(trm, 8, 128),
                                                            in1=bl(at.t[:, ci, g8:g8 + 8], 8, 128), op=ALU.mult),
                          R=[CST.b, at.b], W=[rhs_bc.b])
                    for hf in range(2):
                        kb.mm(psW.t[:, hf * 512:(hf + 1) * 512],
                              [(onesf.t[:, :], rhs_bc.t[:, hf * 4:(hf + 1) * 4, :].rearrange("p r l -> p (r l)"))],
                              R=[onesf.b, rhs_bc.b], W=[psW.b])
                    kb.op("act", lambda e: e.copy(out=T1.t[:, :, :].rearrange("p r l -> p (r l)"), in_=psW.t[:, :]), R=[psW.b], W=[T1.b])
                    kb.op("dve", lambda e: e.tensor_tensor(out=T1.t[:, :, :], in0=T1.t[:, :, :],
                                                            in1=bl(sm.t[:, 0:8], 8, 128), op=ALU.subtract),
                          R=[T1.b, sm.b], W=[T1.b])
                    kb.op("dve", lambda e: e.tensor_tensor(out=T1.t[:, :, :], in0=T1.t[:, :, :], in1=bm(ngm, 8, 128), op=ALU.add),
                          R=[T1.b, CST.b], W=[T1.b])
                    kb.op("act", lambda e: e.activation(out=Eb.t[:, :, :], in_=T1.t[:, :, :], func=AF.Exp), R=[T1.b], W=[Eb.b])
                    psC = next_ps()
                    kb.mm(psC.t[:, 0:128], [(cvT.t[:, 4, cols], cvT.t[:, 5, cols])], R=[cvT.b], W=[psC.b])
                    kb.op("act", lambda e: e.copy(out=cbS.t[:, :], in_=psC.t[:, 0:128]), R=[psC.b], W=[cbS.b])
                    kb.op("dve", lambda e: e.tensor_tensor(out=Mb.t[:, :, :], in0=Eb.t[:, :, :], in1=bm(cbS.t[:, :], 8, 128),
                                                            op=ALU.mult), R=[Eb.b, cbS.b], W=[Mb.b])
                    if not samp:
                        kb.mm(psE.t[:, :], [(cvT.t[:, 5, cols], STb.t[:, :])], R=[cvT.b, STb.b], W=[psE.b])
                    else:
                        kb.op("dve", lambda e: e.tensor_tensor(out=CTz.t[:, :, :], in0=bm(cvT.t[:, 5, cols], 16, 128),
                                                                in1=SEL.t[:, :, :], op=ALU.mult), R=[cvT.b, SEL.b], W=[CTz.b])
                        kb.op("dve", lambda e: e.tensor_tensor(out=rhs_s.t[:, :, :], in0=bm(at.t[:, 4, g8:g8 + 8], 16, 8),
                                                                in1=bl(CST.t[:, C_RM:C_RM + 16], 16, 8), op=ALU.mult),
                              R=[at.b, CST.b], W=[rhs_s.b])
                        psT_ = next_ps()
                        kb.mm(psT_.t[:, 0:128], [(onesf.t[:, :], rhs_s.t[:, :, :].rearrange("p b r -> p (b r)"))],
                              R=[onesf.b, rhs_s.b], W=[psT_.b])
                        kb.op("act", lambda e: e.activation(out=dch_s.t[:, :, :].rearrange("p b r -> p (b r)"),
                                                             in_=psT_.t[:, 0:128], func=AF.Exp), R=[psT_.b], W=[dch_s.b])
                        for b in range(16):
                            sf, sbb = stf[b % 2], stb[b % 2]
                            kb.dma([(sf.t[:, :], stT[b, :, g * 512:(g + 1) * 512])], W=[sf.b])
                            kb.op("act", lambda e: e.copy(out=sbb.t[:, :], in_=sf.t[:, :]), R=[sf.b], W=[sbb.b])
                            kb.mm(psE.t[:, :], [(CTz.t[:, b, :], sbb.t[:, :])], R=[CTz.b, sbb.b], W=[psE.b],
                                  start=(b == 0), stop=(b == 15))
                            bz = Bz[b % 2]
                            kb.op("dve", lambda e: e.tensor_scalar(out=bz.t[:, :], in0=Btok.t[:, :],
                                                                    scalar1=CST.t[:, C_RM + b:C_RM + b + 1], scalar2=None,
                                                                    op0=ALU.mult), R=[Btok.b, CST.b], W=[bz.b])
                            psF = next_ps()
                            kb.mm(psF.t[:, :], [(bz.t[:, :], xdtd.t[:, :])], R=[bz.b, xdtd.b], W=[psF.b])
                            o_ = so[b % 2]
                            kb.op("dve", lambda e: e.tensor_tensor(out=o_.t[:, :].rearrange("p (r d) -> p r d", r=8),
                                                                    in0=sf.t[:, :].rearrange("p (r d) -> p r d", r=8),
                                                                    in1=bl(dch_s.t[:, b, :], 8, 64), op=ALU.mult),
                                  R=[sf.b, dch_s.b], W=[o_.b])
                            kb.op("dve", lambda e: e.tensor_tensor(out=o_.t[:, :], in0=o_.t[:, :], in1=psF.t[:, :], op=ALU.add),
                                  R=[o_.b, psF.b], W=[o_.b])
                            kb.dma([(o_sssm[b, :, g * 512:(g + 1) * 512], o_.t[:, :])], R=[o_.b])
                    psD = next_ps()
                    for r in range(8):
                        kb.mm(psD.t[:, r * 64:(r + 1) * 64], [(Mb.t[:, r, :], xdt.t[:, r * 64:(r + 1) * 64])],
                              R=[Mb.b, xdt.b], W=[psD.b])
                    kb.op("act", lambda e: e.activation(out=sm.t[:, 32:40], in_=sm.t[:, 0:8], func=AF.Exp), R=[sm.b], W=[sm.b])
                    kb.op("act", lambda e: e.copy(out=y1.t[:, :], in_=psE.t[:, :]), R=[psE.b], W=[y1.b])
                    kb.op("dve", lambda e: e.tensor_tensor(out=y1.t[:, :].rearrange("p (r d) -> p r d", r=8),
                                                            in0=y1.t[:, :].rearrange("p (r d) -> p r d", r=8),
                                                            in1=bl(sm.t[:, 32:40], 8, 64), op=ALU.mult),
                          R=[y1.b, sm.b], W=[y1.b])
                    kb.op("dve", lambda e: e.tensor_tensor(out=y1.t[:, :], in0=y1.t[:, :], in1=psD.t[:, :], op=ALU.add),
                          R=[y1.b, psD.b], W=[y1.b])
                    kb.op("dve", lambda e: e.tensor_tensor(out=y2.t[:, :].rearrange("p (r d) -> p r d", r=8),
                                                            in0=xstok.t[:, :].rearrange("p (r d) -> p r d", r=8),
                                                            in1=bl(TK.t[:, 64 + g8:64 + g8 + 8], 8, 64), op=ALU.mult),
                          R=[xstok.b, TK.b], W=[y2.b])
                    kb.op("dve", lambda e: e.tensor_tensor(out=y1.t[:, :], in0=y1.t[:, :], in1=y2.t[:, :], op=ALU.add),
                          R=[y1.b, y2.b], W=[y1.b])
                    kb.op("dve", lambda e: e.tensor_tensor(out=y1.t[:, :], in0=y1.t[:, :], in1=sztok.t[:, :], op=ALU.mult),
                          R=[y1.b, sztok.b], W=[y1.b])
                    kb.op("act", lambda e: e.activation(out=y2.t[:, :], in_=y1.t[:, :], func=AF.Square, accum_out=sm.t[:, 40:41]),
                          R=[y1.b], W=[y2.b, sm.b])
                    kb.op("act", lambda e: e.activation(out=sm.t[:, 41:42], in_=sm.t[:, 40:41], func=AF.Ln, bias=EPS, scale=1.0 / 512.0),
                          R=[sm.b], W=[sm.b])
                    kb.op("act", lambda e: e.activation(out=sm.t[:, 42:43], in_=sm.t[:, 41:42], func=AF.Exp, scale=-0.5),
                          R=[sm.b], W=[sm.b])
                    kb.op("dve", lambda e: e.tensor_scalar(out=ynb.t[:, :], in0=y1.t[:, :], scalar1=sm.t[:, 42:43], scalar2=None,
                                                            op0=ALU.mult), R=[y1.b, sm.b], W=[ynb.b])
                    kb.tr([(pst[1].t[:, k * 128:(k + 1) * 128], ynb.t[:, k * 128:(k + 1) * 128]) for k in range(4)], identb.t[:, :],
                          R=[ynb.b, identb.b], W=[pst[1].b])
                    for k in range(4):
                        kb.op("dve", lambda e: e.tensor_scalar(out=yTblk.t[:, k, cols], in0=pst[1].t[:, k * 128:(k + 1) * 128],
                                                                scalar1=PV.t[:, PV_SSDN + g * 4 + k:PV_SSDN + g * 4 + k + 1],
                                                                scalar2=None, op0=ALU.mult), R=[pst[1].b, PV.b], W=[yTblk.b])
                if not samp:
                    psF = next_ps()
                    kb.mm(psF.t[:, :], [(Btok.t[:, :], xdtd.t[:, :])], R=[Btok.b, xdtd.b], W=[psF.b])
                    kb.op("dve", lambda e: e.tensor_tensor(out=ST.rearrange("p (r d) -> p r d", r=8),
                                                            in0=ST.rearrange("p (r d) -> p r d", r=8),
                                                            in1=bl(sm.t[:, 24:32], 8, 64), op=ALU.mult),
                          R=[STall.b, sm.b], W=[STall.b])
                    kb.op("dve", lambda e: e.tensor_tensor(out=ST, in0=ST, in1=psF.t[:, :], op=ALU.add),
                          R=[STall.b, psF.b], W=[STall.b])
                    kb.op("act", lambda e: e.copy(out=STb.t[:, :], in_=ST), R=[STall.b], W=[STb.b])
                if KSUB == 4:
                    return
            if full and KDBG and samp_seg and g == 0:
                kb.op("dve", lambda e: e.tensor_copy(out=y1.t[:, :].rearrange("p (a b) -> p a b", a=4), in_=yTblk.t[:, :, 512:640]),
                      R=[yTblk.b], W=[y1.b])
                kb.dma([(dbg1[:, :], y1.t[:, :])], R=[y1.b])
            if full:
                wout_apply(g, tbs)

    def sc_seg(first_main, samp_seg):
        tbs = TBF if samp_seg else TBP
        for j in range(16):
            sh = load_w(wsch[j])
            sc_ = load_w(wscc[j])
            if first_main:
                p1 = next_ps()
                kb.mm(p1.t[:, 0:2], [(sh.t[:, c, :], xnl.t[:, c, :]) for c in range(16)], R=[sh.b, xnl.b], W=[p1.b])
                kb.mm(p1.t[:, 2:4], [(sc_.t[:, c, :], xnl.t[:, c, :]) for c in range(16)], R=[sc_.b, xnl.b], W=[p1.b])
                kb.op("act", lambda e: e.copy(out=hl.t[:, :], in_=p1.t[:, 0:2]), R=[p1.b], W=[hl.b])
                kb.op("dve", lambda e: e.tensor_tensor(out=halo_sc.t[:, j, :], in0=hl.t[:, :], in1=p1.t[:, 2:4], op=ALU.mult),
                      R=[hl.b, p1.b], W=[halo_sc.b])
            phs, pcs = [], []
            for bi, (t0, n) in enumerate(tbs):
                ph, pc = next_ps(), None
                kb.mm(ph.t[:, 0:n], [(sh.t[:, c, :], xnT.t[:, c, t0:t0 + n]) for c in range(16)], R=[sh.b, xnT.b], W=[ph.b])
                kb.op("act", lambda e: e.copy(out=hsb.t[:, 0:n], in_=ph.t[:, 0:n]), R=[ph.b], W=[hsb.b])
                pc = next_ps()
                kb.mm(pc.t[:, 0:n], [(sc_.t[:, c, :], xnT.t[:, c, t0:t0 + n]) for c in range(16)], R=[sc_.b, xnT.b], W=[pc.b])
                cnt["h"] += 1
                a_ = acc[cnt["h"] % 2]
                if t0 < 512:
                    p_ = pb[cnt["h"] % 2]
                    kb.op("dve", lambda e: e.tensor_tensor(out=p_.t[:, 2:2 + n], in0=hsb.t[:, 0:n], in1=pc.t[:, 0:n], op=ALU.mult),
                          R=[hsb.b, pc.b], W=[p_.b])
                    kb.op("dve", lambda e: e.tensor_copy(out=p_.t[:, 0:2], in_=halo_sc.t[:, j, :]), R=[halo_sc.b], W=[p_.b])
                    kb.op("dve", lambda e: e.tensor_copy(out=halo_sc.t[:, j, :], in_=p_.t[:, 512:514]), R=[p_.b], W=[halo_sc.b])
                    kb.op("dve", lambda e: e.tensor_scalar(out=a_.t[:, 0:n], in0=p_.t[:, 0:n],
                                                            scalar1=PV.t[:, PV_SCW + j * 3:PV_SCW + j * 3 + 1], scalar2=None,
                                                            op0=ALU.mult), R=[p_.b, PV.b], W=[a_.b])
                    for q in range(1, 3):
                        kb.op("dve", lambda e: e.scalar_tensor_tensor(
                            out=a_.t[:, 0:n], in0=p_.t[:, q:q + n], scalar=PV.t[:, PV_SCW + j * 3 + q:PV_SCW + j * 3 + q + 1],
                            in1=a_.t[:, 0:n], op0=ALU.mult, op1=ALU.add), R=[p_.b, PV.b, a_.b], W=[a_.b])
                else:
                    kb.op("act", lambda e: e.copy(out=a_.t[:, 0:128], in_=pc.t[:, 0:128]), R=[pc.b], W=[a_.b])
                    kb.op("dve", lambda e: e.tensor_tensor(out=pbs.t[:, :, 2:10],
                                                            in0=hsb.t[:, 0:128].rearrange("p (b t) -> p b t", b=16),
                                                            in1=a_.t[:, 0:128].rearrange("p (b t) -> p b t", b=16), op=ALU.mult),
                          R=[hsb.b, a_.b], W=[pbs.b])
                    kb.op("dve", lambda e: e.tensor_copy(out=pbs.t[:, :, 0:2], in_=hists.t[:, j, :].rearrange("p (b t) -> p b t", b=16)),
                          R=[hists.b], W=[pbs.b])
                    kb.op("dve", lambda e: e.tensor_copy(out=osc_s.t[:, j, :, :], in_=pbs.t[:, :, 8:10]), R=[pbs.b], W=[osc_s.b])
                    av = a_.t[:, 0:128].rearrange("p (b t) -> p b t", b=16)
                    kb.op("dve", lambda e: e.tensor_scalar(out=av, in0=pbs.t[:, :, 0:8],
                                                            scalar1=PV.t[:, PV_SCW + j * 3:PV_SCW + j * 3 + 1], scalar2=None,
                                                            op0=ALU.mult), R=[pbs.b, PV.b], W=[a_.b])
                    for q in range(1, 3):
                        kb.op("dve", lambda e: e.scalar_tensor_tensor(
                            out=av, in0=pbs.t[:, :, q:q + 8], scalar=PV.t[:, PV_SCW + j * 3 + q:PV_SCW + j * 3 + q + 1],
                            in1=av, op0=ALU.mult, op1=ALU.add), R=[pbs.b, PV.b, a_.b], W=[a_.b])
                kb.op("dve", lambda e: e.tensor_copy(out=yTblk.t[:, j % 4, t0:t0 + n], in_=a_.t[:, 0:n]), R=[a_.b], W=[yTblk.b])
            sb_ = load_w(wscb[j])
            for bi, (t0, n) in enumerate(tbs):
                pbb = next_ps()
                kb.mm(pbb.t[:, 0:n], [(sb_.t[:, c, :], xnT.t[:, c, t0:t0 + n]) for c in range(16)], R=[sb_.b, xnT.b], W=[pbb.b])
                kb.op("dve", lambda e: e.tensor_tensor(out=yTblk.t[:, j % 4, t0:t0 + n], in0=yTblk.t[:, j % 4, t0:t0 + n],
                                                        in1=pbb.t[:, 0:n], op=ALU.mult), R=[yTblk.b, pbb.b], W=[yTblk.b])
            if KDBG and samp_seg and j == 3:
                kb.op("dve", lambda e: e.tensor_copy(out=hsb.t[:, :].rearrange("p (a b) -> p a b", a=4), in_=yTblk.t[:, :, 512:640]),
                      R=[yTblk.b], W=[hsb.b])
                kb.dma([(dbg2[:, :], hsb.t[:, :])], R=[hsb.b])
            if j % 4 == 3:
                wout_apply(4 + j // 4, tbs)


    SCALE = 1.0 / math.sqrt(512.0)

    def attention_seg(first, samp_seg, ntok):
        tbs = TBF if samp_seg else TBP
        AT = ExitStack()
        kTp = kb.sb(AT, "kTp", [128, 16, 256], BF16)
        vp = kb.sb(AT, "vp", [128, 2, 2048], BF16)
        if not first:
            kb.dma([(kTp.t[:, :, :].rearrange("p a b -> p (a b)"), KVd[:, 0:4096]),
                    (vp.t[:, :, :].rearrange("p a b -> p (a b)"), KVd[:, 4096:8192])], W=[kTp.b, vp.b], owner=kTp.b)
        else:
            KV = ExitStack()
            MT = kb.sb(KV, "MT", [128, 16, 256])
            mnT = kb.sb(KV, "mnT", [128, 16, 256], BF16)
            vTb = kb.sb(KV, "vTb", [128, 16, 256], BF16)
            stg = [kb.sb(KV, f"stg{i}", [128, 256]) for i in range(2)]
            kb.dma([(MT.t[:, 4 * i:4 * i + 4, :], memT[512 * i:512 * (i + 1), :].rearrange("(c p) t -> p c t", p=128))
                    for i in range(4)], W=[MT.b])
            S_ = ExitStack()
            rmsnormT(MT, 256, PV_GMKV, mnT, S_)
            kb.barrier()
            S_.close()

            def mk_ev(dstT, odram):
                def ev(j, bi, t0, n, ps):
                    o = stg[j % 2]
                    kb.op("act", lambda e: e.copy(out=o.t[:, :], in_=ps.t[:, 0:256]), R=[ps.b], W=[o.b])
                    kb.op("dve", lambda e: e.tensor_copy(out=dstT.t[:, j, :], in_=o.t[:, :]), R=[o.b], W=[dstT.b])
                    if first:
                        kb.dma([(odram[j * 128:(j + 1) * 128, :], o.t[:, :])], R=[o.b])
                return ev
            linearT([wmk[j] for j in range(16)], mnT, [(0, 256)], mk_ev(kTp, o_mk))
            linearT([wmv[j] for j in range(16)], mnT, [(0, 256)], mk_ev(vTb, o_mv))
            for j in range(16):
                pt = pst[j % 2]
                kb.tr([(pt.t[:, mt * 128:(mt + 1) * 128], vTb.t[:, j, mt * 128:(mt + 1) * 128]) for mt in range(2)],
                      identb.t[:, :], R=[vTb.b, identb.b], W=[pt.b])
                for mt in range(2):
                    kb.op("act", lambda e: e.copy(out=vp.t[:, mt, j * 128:(j + 1) * 128], in_=pt.t[:, mt * 128:(mt + 1) * 128]),
                          R=[pt.b], W=[vp.b])
            kb.barrier()
            KV.close()
            kb.dma([(KVd[:, 0:4096], kTp.t[:, :, :].rearrange("p a b -> p (a b)")),
                    (KVd[:, 4096:8192], vp.t[:, :, :].rearrange("p a b -> p (a b)"))], R=[kTp.b, vp.b])
        qT = kb.sb(AT, "qT", [128, 16, SEGW], BF16)
        oT = xnT
        eT = [kb.sb(AT, f"eT{i}", [128, 512], BF16) for i in range(2)]
        rz = kb.sb(AT, "rz", [128, 512])
        S_ = ExitStack()
        rmsnormT(XT, ntok, PV_GMQ, xnT, S_)
        kb.barrier()
        S_.close()

        def ev_q(j, bi, t0, n, ps):
            kb.op("act", lambda e: e.copy(out=qT.t[:, j, t0:t0 + n], in_=ps.t[:, 0:n]), R=[ps.b], W=[qT.b])
        linearT([wmq[j] for j in range(16)], xnT, tbs, ev_q)
        for h in range(4):
            for mt in range(2):
                ps = next_ps()
                kb.mm(ps.t[:, 0:512], [(kTp.t[:, h * 4 + dc, mt * 128:(mt + 1) * 128], qT.t[:, h * 4 + dc, 0:512]) for dc in range(4)],
                      R=[kTp.b, qT.b], W=[ps.b])
                kb.op("act", lambda e: e.activation(out=eT[mt].t[:, :], in_=ps.t[:, 0:512], func=AF.Exp, scale=SCALE),
                      R=[ps.b], W=[eT[mt].b])
            ps = next_ps()
            kb.mm(ps.t[:, 0:512], [(onesb.t[:, :], eT[mt].t[:, :]) for mt in range(2)], R=[onesb.b, eT[0].b, eT[1].b], W=[ps.b])
            kb.op("dve", lambda e: e.reciprocal(out=rz.t[:, :], in_=ps.t[:, 0:512]), R=[ps.b], W=[rz.b])
            for dvt in range(4):
                ps = next_ps()
                c_ = h * 512 + dvt * 128
                kb.mm(ps.t[:, 0:512], [(vp.t[:, mt, c_:c_ + 128], eT[mt].t[:, :]) for mt in range(2)],
                      R=[vp.b, eT[0].b, eT[1].b], W=[ps.b])
                kb.op("dve", lambda e: e.tensor_tensor(out=oT.t[:, h * 4 + dvt, 0:512], in0=ps.t[:, 0:512], in1=rz.t[:, :], op=ALU.mult),
                      R=[ps.b, rz.b], W=[oT.b])
        if samp_seg:
            kbf = [kb.sb(AT, f"kbf{i}", [128, 4, 256], BF16) for i in range(4)]
            vbf = [kb.sb(AT, f"vbf{i}", [128, 2, 512], BF16) for i in range(4)]
            es_ = kb.sb(AT, "es_", [128, 16], BF16)
            rzs = kb.sb(AT, "rzs", [128, 8])
            it = 0
            for b in range(16):
                tc0 = 512 + 8 * b
                for h in range(4):
                    i = it % 4
                    it += 1
                    kb.dma([(kbf[i].t[:, :, :], ckT[b, h * 512:(h + 1) * 512, :].rearrange("(c p) m -> p c m", p=128))], W=[kbf[i].b],
                           q="pool")
                    kb.dma([(vbf[i].t[:, :, :], cv[b, :, h * 512:(h + 1) * 512].rearrange("(mt p) d -> p mt d", p=128))], W=[vbf[i].b],
                           q="pool")
                    ps = next_ps()
                    for mt in range(2):
                        kb.mm(ps.t[:, mt * 8:(mt + 1) * 8],
                              [(kbf[i].t[:, dc, mt * 128:(mt + 1) * 128], qT.t[:, h * 4 + dc, tc0:tc0 + 8]) for dc in range(4)],
                              R=[kbf[i].b, qT.b], W=[ps.b])
                    kb.op("act", lambda e: e.activation(out=es_.t[:, :], in_=ps.t[:, 0:16], func=AF.Exp, scale=SCALE),
                          R=[ps.b], W=[es_.b])
                    ps2 = next_ps()
                    kb.mm(ps2.t[:, 0:8], [(onesb.t[:, :], es_.t[:, mt * 8:(mt + 1) * 8]) for mt in range(2)],
                          R=[onesb.b, es_.b], W=[ps2.b])
                    for dvt in range(4):
                        kb.mm(ps2.t[:, 8 + dvt * 8:16 + dvt * 8],
                              [(vbf[i].t[:, mt, dvt * 128:(dvt + 1) * 128], es_.t[:, mt * 8:(mt + 1) * 8]) for mt in range(2)],
                              R=[vbf[i].b, es_.b], W=[ps2.b])
                    kb.op("dve", lambda e: e.reciprocal(out=rzs.t[:, :], in_=ps2.t[:, 0:8]), R=[ps2.b], W=[rzs.b])
                    for dvt in range(4):
                        kb.op("dve", lambda e: e.tensor_tensor(out=oT.t[:, h * 4 + dvt, tc0:tc0 + 8], in0=ps2.t[:, 8 + dvt * 8:16 + dvt * 8],
                                                                in1=rzs.t[:, :], op=ALU.mult), R=[ps2.b, rzs.b], W=[oT.b])

        def ev_o(j, bi, t0, n, ps):
            kb.op("dve", lambda e: e.tensor_tensor(out=XT.t[:, j, t0:t0 + n], in0=XT.t[:, j, t0:t0 + n], in1=ps.t[:, 0:n],
                                                    op=ALU.add), R=[XT.b, ps.b], W=[XT.b])
        linearT([wmo[j] for j in range(16)], oT, tbs, ev_o)
        kb.barrier()
        AT.close()

    def peer_seg(first, samp_seg, ntok):
        tbs = TBF if samp_seg else TBP
        ntile = ntok // 128
        P0 = ExitStack()
        asel_all = kb.sb(P0, "asel_all", [128, 5, 128])
        bsel_all = kb.sb(P0, "bsel_all", [128, 5, 128])
        gts_all = kb.sb(P0, "gts_all", [128, 5, 128])
        PS_ = ExitStack()
        skb = kb.sb(PS_, "skb", [128, 16, 128], BF16)
        T_ = ExitStack()
        skf = kb.sb(T_, "skf", [128, 16, 128])
        kb.dma([(skf.t[:, :, :], skT.rearrange("a p k -> p a k"))], W=[skf.b])
        kb.op("dve", lambda e: e.tensor_copy(out=skb.t[:, :, :], in_=skf.t[:, :, :]), R=[skf.b], W=[skb.b])
        kb.barrier()
        T_.close()
        qpT = kb.sb(PS_, "qpT", [128, 16, SEGW], BF16)
        S_ = ExitStack()
        rmsnormT(XT, ntok, PV_GFFN, xnT, S_)
        kb.barrier()
        S_.close()

        def ev_q(j, bi, t0, n, ps):
            kb.op("act", lambda e: e.copy(out=qpT.t[:, j, t0:t0 + n], in_=ps.t[:, 0:n]), R=[ps.b], W=[qpT.b])
        linearT([wpq[j] for j in range(16)], xnT, tbs, ev_q)
        Ssc = kb.sb(PS_, "Ssc", [128, 16, 128])
        Sw = kb.sb(PS_, "Sw", [128, 256])
        V1 = kb.sb(PS_, "V1", [128, 16, 16])
        I1 = kb.sb(PS_, "I1", [128, 16, 16], U32)
        I1f = kb.sb(PS_, "I1f", [128, 16, 16])
        comb = kb.sb(PS_, "comb", [128, 8, 256])
        Fv = kb.sb(PS_, "Fv", [128, 8, 16])
        PI = kb.sb(PS_, "PI", [128, 8, 16], U32)
        PJ = kb.sb(PS_, "PJ", [128, 8, 16], U32)
        PIf = kb.sb(PS_, "PIf", [128, 8, 16])
        jf = kb.sb(PS_, "jf", [128, 8, 16])
        jpf = kb.sb(PS_, "jpf", [128, 8, 16])
        oh = kb.sb(PS_, "oh", [128, 16, 16])
        gsm = kb.sb(PS_, "gsm", [128, 8])
        IO128 = kb.sb(PS_, "IO128", [128, 128])
        kb.dma([(IO128.t[:, :], cst[:, C_IO128:C_IO128 + 128])], W=[IO128.b])
        trb = kb.sb(PS_, "trb", [128, 3, 128], BF16)
        abT = kb.sb(PS_, "abT", [128, 3, 128])
        OHB = kb.sb(PS_, "OHB", [128, 32, 128], BF16)
        OHA = kb.sb(PS_, "OHA", [128, 32, 128], BF16)
        GS = kb.sb(PS_, "GS", [128, 128, 128], BF16)
        for tt in range(ntile):
            cols = slice(tt * 128, (tt + 1) * 128)
            asel = asel_all.t[:, tt, :].rearrange("p (a b) -> p a b", a=8)
            bsel = bsel_all.t[:, tt, :].rearrange("p (a b) -> p a b", a=8)
            gts = gts_all.t[:, tt, :].rearrange("p (a b) -> p a b", a=8)
            for q4 in range(4):
                ps = next_ps()
                for k in range(4):
                    hi = q4 * 4 + k
                    kb.mm(ps.t[:, k * 128:(k + 1) * 128], [(qpT.t[:, hi, cols], skb.t[:, hi, :])], R=[qpT.b, skb.b], W=[ps.b])
                kb.op("act", lambda e: e.copy(out=Ssc.t[:, q4 * 4:(q4 + 1) * 4, :].rearrange("p a b -> p (a b)"), in_=ps.t[:, 0:512]),
                      R=[ps.b], W=[Ssc.b])
            for hi in range(16):
                kb.op("dve", lambda e: e.max(out=V1.t[:, hi, 0:8], in_=Ssc.t[:, hi, :]), R=[Ssc.b], W=[V1.b])
                kb.op("dve", lambda e: e.max_index(out=I1.t[:, hi, 0:8], in_max=V1.t[:, hi, 0:8], in_values=Ssc.t[:, hi, :]),
                      R=[V1.b, Ssc.b], W=[I1.b])
                kb.op("dve", lambda e: e.match_replace(out=Sw.t[:, 0:128], in_to_replace=V1.t[:, hi, 0:8], in_values=Ssc.t[:, hi, :],
                                                        imm_value=-1e30), R=[V1.b, Ssc.b], W=[Sw.b])
                kb.op("dve", lambda e: e.max(out=V1.t[:, hi, 8:16], in_=Sw.t[:, 0:128]), R=[Sw.b], W=[V1.b])
                kb.op("dve", lambda e: e.max_index(out=I1.t[:, hi, 8:16], in_max=V1.t[:, hi, 8:16], in_values=Sw.t[:, 0:128]),
                      R=[V1.b, Sw.b], W=[I1.b])
            kb.op("dve", lambda e: e.tensor_copy(out=I1f.t[:, :, :], in_=I1.t[:, :, :]), R=[I1.b], W=[I1f.b])
            for h in range(8):
                kb.op("dve", lambda e: e.tensor_tensor(out=comb.t[:, h, :].rearrange("p (a b) -> p a b", a=16),
                                                        in0=bl(V1.t[:, 2 * h, :], 16, 16), in1=bm(V1.t[:, 2 * h + 1, :], 16, 16),
                                                        op=ALU.add), R=[V1.b], W=[comb.b])
                kb.op("dve", lambda e: e.max(out=Fv.t[:, h, 0:8], in_=comb.t[:, h, :]), R=[comb.b], W=[Fv.b])
                kb.op("dve", lambda e: e.max_index(out=PI.t[:, h, 0:8], in_max=Fv.t[:, h, 0:8], in_values=comb.t[:, h, :]),
                      R=[Fv.b, comb.b], W=[PI.b])
                kb.op("dve", lambda e: e.match_replace(out=Sw.t[:, :], in_to_replace=Fv.t[:, h, 0:8], in_values=comb.t[:, h, :],
                                                        imm_value=-1e30), R=[Fv.b, comb.b], W=[Sw.b])
                kb.op("dve", lambda e: e.max(out=Fv.t[:, h, 8:16], in_=Sw.t[:, :]), R=[Sw.b], W=[Fv.b])
                kb.op("dve", lambda e: e.max_index(out=PI.t[:, h, 8:16], in_max=Fv.t[:, h, 8:16], in_values=Sw.t[:, :]),
                      R=[Fv.b, Sw.b], W=[PI.b])
            kb.op("dve", lambda e: e.tensor_tensor(out=gts, in0=Fv.t[:, :, :], in1=bl(Fv.t[:, :, 0], 8, 16), op=ALU.subtract),
                  R=[Fv.b], W=[gts_all.b])
            for h in range(8):
                kb.op("act", lambda e: e.activation(out=gts[:, h, :], in_=gts[:, h, :], func=AF.Exp, accum_out=gsm.t[:, h:h + 1]),
                      R=[gts_all.b], W=[gts_all.b, gsm.b])
            kb.op("dve", lambda e: e.reciprocal(out=gsm.t[:, :], in_=gsm.t[:, :]), R=[gsm.b], W=[gsm.b])
            kb.op("dve", lambda e: e.tensor_tensor(out=gts, in0=gts, in1=bl(gsm.t[:, :], 8, 16), op=ALU.mult),
                  R=[gts_all.b, gsm.b], W=[gts_all.b])
            kb.op("dve", lambda e: e.tensor_scalar(out=PJ.t[:, :, :], in0=PI.t[:, :, :], scalar1=4, scalar2=None,
                                                    op0=ALU.logical_shift_right), R=[PI.b], W=[PJ.b])
            kb.op("dve", lambda e: e.tensor_copy(out=jf.t[:, :, :], in_=PJ.t[:, :, :]), R=[PJ.b], W=[jf.b])
            kb.op("dve", lambda e: e.tensor_copy(out=PIf.t[:, :, :], in_=PI.t[:, :, :]), R=[PI.b], W=[PIf.b])
            kb.op("dve", lambda e: e.scalar_tensor_tensor(out=jpf.t[:, :, :], in0=jf.t[:, :, :], scalar=-16.0, in1=PIf.t[:, :, :],
                                                           op0=ALU.mult, op1=ALU.add), R=[jf.b, PIf.b], W=[jpf.b])
            for h in range(8):
                for (src, hi, dst, dstb) in ((jf, 2 * h, asel, asel_all), (jpf, 2 * h + 1, bsel, bsel_all)):
                    kb.op("dve", lambda e: e.tensor_tensor(out=oh.t[:, :, :], in0=bl(src.t[:, h, :], 16, 16), in1=bm(IOT.t[:, :], 16, 16),
                                                            op=ALU.is_equal), R=[src.b, IOT.b], W=[oh.b])
                    kb.op("dve", lambda e: e.tensor_tensor(out=oh.t[:, :, :], in0=oh.t[:, :, :], in1=bm(I1f.t[:, hi, :], 16, 16),
                                                            op=ALU.mult), R=[oh.b, I1f.b], W=[oh.b])
                    kb.op("dve", lambda e: e.reduce_sum(out=dst[:, h, :], in_=oh.t[:, :, :], axis=mybir.AxisListType.X),
                          R=[oh.b], W=[dstb.b])
            for w_, srcall in enumerate((asel_all, bsel_all, gts_all)):
                kb.op("dve", lambda e: e.tensor_copy(out=trb.t[:, w_, :], in_=srcall.t[:, tt, :]), R=[srcall.b], W=[trb.b])
            kb.tr([(pst[0].t[:, w_ * 128:(w_ + 1) * 128], trb.t[:, w_, :]) for w_ in range(3)], identb.t[:, :],
                  R=[trb.b, identb.b], W=[pst[0].b])
            kb.op("act", lambda e: e.copy(out=abT.t[:, :, :].rearrange("p a b -> p (a b)"), in_=pst[0].t[:, 0:384]),
                  R=[pst[0].b], W=[abT.b])
            for qt in range(4):
                hs_ = slice(qt * 32, (qt + 1) * 32)
                kb.op("dve", lambda e: e.tensor_tensor(out=OHB.t[:, :, :], in0=bl(abT.t[:, 1, hs_], 32, 128), in1=bm(IO128.t[:, :], 32, 128),
                                                        op=ALU.is_equal), R=[abT.b, IO128.b], W=[OHB.b])
                kb.op("dve", lambda e: e.tensor_tensor(out=OHA.t[:, :, :], in0=bl(abT.t[:, 0, hs_], 32, 128), in1=bm(IO128.t[:, :], 32, 128),
                                                        op=ALU.is_equal), R=[abT.b, IO128.b], W=[OHA.b])
                kb.op("dve", lambda e: e.tensor_tensor(out=OHA.t[:, :, :], in0=OHA.t[:, :, :], in1=bl(abT.t[:, 2, hs_], 32, 128),
                                                        op=ALU.mult), R=[OHA.b, abT.b], W=[OHA.b])
                for t4 in range(8):
                    ps = next_ps()
                    for q in range(4):
                        tl = t4 * 4 + q
                        kb.mm(ps.t[:, q * 128:(q + 1) * 128], [(OHB.t[:, tl, :], OHA.t[:, tl, :])], R=[OHB.b, OHA.b], W=[ps.b])
                    for q in range(4):
                        t = qt * 32 + t4 * 4 + q
                        kb.op("act", lambda e: e.copy(out=GS.t[:, :, t], in_=ps.t[:, q * 128:(q + 1) * 128]), R=[ps.b], W=[GS.b])
            kb.dma([(Gd[16 * i:16 * (i + 1), :, tt * 128:(tt + 1) * 128].rearrange("a b t -> b a t"), GS.t[:, 16 * i:16 * (i + 1), :])
                    for i in range(8)], R=[GS.b])
        kb.barrier()
        PS_.close()
        P3 = ExitStack()
        ubs = [kb.sb(P3, f"ub{i}", [128, 2048], BF16) for i in range(2)]
        uTs = [kb.sb(P3, f"uT{i}", [128, 16, 128], BF16) for i in range(2)]
        vb = [kb.sb(P3, f"vb{i}", [128, 2048], BF16) for i in range(16)]
        Ga = [kb.sb(P3, f"Ga{i}", [128, SEGW], BF16) for i in range(2)]
        hgs = [kb.sb(P3, f"hg{i}", [128, SEGW], BF16) for i in range(2)]
        Aa = [kb.sb(P3, f"Aa{i}", [128, SEGW], BF16) for i in range(8)]
        for a in range(128):
            i2, i8 = a % 2, a % 8
            i16 = a % 16
            ub = ubs[i2]
            uT = uTs[i2]
            hg = hgs[i2]
            kb.dma([(ub.t[:, :], pu[a * 128:(a + 1) * 128, :])], W=[ub.b], q="pool")
            for q2 in range(2):
                pt = pst[q2]
                kb.tr([(pt.t[:, k * 128:(k + 1) * 128], ub.t[:, (q2 * 8 + k) * 128:(q2 * 8 + k + 1) * 128]) for k in range(8)],
                      identb.t[:, :], R=[ub.b, identb.b], W=[pt.b])
                kb.op("act", lambda e: e.copy(out=uT.t[:, q2 * 8:(q2 + 1) * 8, :].rearrange("p a b -> p (a b)"), in_=pt.t[:, :]),
                      R=[pt.b], W=[uT.b])
            kb.dma([(vb[i16].t[:, :], pv[a * 128:(a + 1) * 128, :])], W=[vb[i16].b], q="pool")
            kb.dma([(Ga[i2].t[:, 0:ntok], Gd[a, :, 0:ntok])], W=[Ga[i2].b])
            for (t0, n) in tbs:
                ps = next_ps()
                kb.mm(ps.t[:, 0:n], [(uT.t[:, c, :], xnT.t[:, c, t0:t0 + n]) for c in range(16)], R=[uT.b, xnT.b], W=[ps.b])
                kb.op("act", lambda e: e.activation(out=hg.t[:, t0:t0 + n], in_=ps.t[:, 0:n], func=AF.Gelu), R=[ps.b], W=[hg.b])
                kb.op("dve", lambda e: e.tensor_tensor(out=Aa[i8].t[:, t0:t0 + n], in0=hg.t[:, t0:t0 + n], in1=Ga[i2].t[:, t0:t0 + n],
                                                        op=ALU.mult), R=[hg.b, Ga[i2].b], W=[Aa[i8].b])
            if i8 == 7:
                for dj in range(16):
                    for (t0, n) in tbs:
                        ps = next_ps()
                        vo = (a // 8 % 2) * 8
                        kb.mm(ps.t[:, 0:n], [(vb[vo + s_].t[:, dj * 128:(dj + 1) * 128], Aa[s_].t[:, t0:t0 + n]) for s_ in range(8)],
                              R=[x.b for x in vb[vo:vo + 8]] + [x.b for x in Aa], W=[ps.b])
                        kb.op("dve", lambda e: e.tensor_tensor(out=XT.t[:, dj, t0:t0 + n], in0=XT.t[:, dj, t0:t0 + n], in1=ps.t[:, 0:n],
                                                                op=ALU.add), R=[XT.b, ps.b], W=[XT.b])
        kb.barrier()
        P3.close()
        P0.close()

    segs = [("pre", xpT, 0, 512, False), ("pre", xpT, 512, 512, False),
            ("main", xT, 0, 512, False), ("main", xT, 512, 640, True)]
    for si, (kind, src, c0, ntok, has_s) in enumerate(segs):
        full = kind == "main"
        MX = ExitStack()
        if has_s:
            histx = kb.sb(MX, "histx", [128, 24, 48])
            hists = kb.sb(MX, "hists", [128, 16, 32])
            kb.dma([(histx.t[:, :, :], stconvT.rearrange("t p f -> p t f")), (hists.t[:, :, :], stscT.rearrange("t p f -> p t f"))],
                   W=[histx.b, hists.b], owner=histx.b)
            oconv_s = kb.sb(MX, "oconv_s", [128, 24, 16, 3])
            osc_s = kb.sb(MX, "osc_s", [128, 16, 16, 2])
        xdt = kb.sb(MX, "xdt", [128, 512], BF16)
        xdtd = kb.sb(MX, "xdtd", [128, 512], BF16)
        Btok = kb.sb(MX, "Btok", [128, 128], BF16)
        xstok = kb.sb(MX, "xstok", [128, 512], BF16)
        sztok = kb.sb(MX, "sztok", [128, 512], BF16)
        sm = kb.sb(MX, "sm", [128, 64])
        T1 = kb.sb(MX, "T1", [128, 8, 128])
        rhs_bc = T1
        Eb = kb.sb(MX, "Eb", [128, 8, 128], BF16)
        Mb = kb.sb(MX, "Mb", [128, 8, 128], BF16)
        y1 = kb.sb(MX, "y1", [128, 512])
        ynb = kb.sb(MX, "ynb", [128, 512], BF16)
        STb = kb.sb(MX, "STb", [128, 512], BF16)
        yTblk = kb.sb(MX, "yTblk", [128, 4, SEGW], BF16)
        hb = [kb.sb(MX, f"hb{i}", [128, 3 + 512]) for i in range(2)]
        acc = [kb.sb(MX, f"acc{i}", [128, 512]) for i in range(2)]
        y2 = acc[0]
        hs = kb.sb(MX, "hs", [128, 16, 11])
        cvT = kb.sb(MX, "cvT", [128, 6, SEGW], BF16)
        szT = kb.sb(MX, "szT", [128, 4, SEGW], BF16)
        CTz = kb.sb(MX, "CTz", [128, 16, 128], BF16)
        stf = [kb.sb(MX, f"stf{i}", [128, 512]) for i in range(2)]
        stb = [kb.sb(MX, f"stb{i}", [128, 512], BF16) for i in range(2)]
        Bz = [kb.sb(MX, f"Bz{i}", [128, 128], BF16) for i in range(2)]
        so = [kb.sb(MX, f"so{i}", [128, 512]) for i in range(2)]
        rhs_s = kb.sb(MX, "rhs_s", [128, 16, 8])
        dch_s = kb.sb(MX, "dch_s", [128, 16, 8])
        dtt = kb.sb(MX, "dtt", [128, 5, 32])
        at = kb.sb(MX, "at", [128, 5, 32])
        hsb = kb.sb(MX, "hsb", [128, 512])
        pb = [kb.sb(MX, f"pb{i}", [128, 2 + 512]) for i in range(2)]
        pbs = kb.sb(MX, "pbs", [128, 16, 10])
        hl = kb.sb(MX, "hl", [128, 2])
        cbS = kb.sb(MX, "cbS", [128, 128])

        kb.dma([(XT.t[:, 4 * i:4 * i + 4, 0:ntok], src[512 * i:512 * (i + 1), c0:c0 + ntok].rearrange("(c p) t -> p c t", p=128))
                for i in range(4)], W=[XT.b])
        if KSTOP == 0:
            break
        S1 = ExitStack()
        rmsnormT(XT, ntok, PV_GMIX, xnT, S1)
        kb.barrier()
        S1.close()
        if KSTOP == 1:
            break
        if si == 1:
            kb.op("dve", lambda e: e.tensor_copy(out=xnl.t[:, :, :], in_=xnT.t[:, :, 510:512]), R=[xnT.b], W=[xnl.b])
        dt_pass(5 if has_s else 4)
        if KSTOP == 2:
            break
        ssd_seg(full, has_s, si == 3)
        if KSTOP == 3 + si:
            break
        if si == 1:
            for g in range(4):
                kb.op("dve", lambda e: e.tensor_scalar(out=STall.t[:, g, :], in0=STall.t[:, g, :], scalar1=FL.t[:, 0:1],
                                                        scalar2=None, op0=ALU.mult), R=[STall.b, FL.b], W=[STall.b])
        if full:
            sc_seg(si == 2, has_s)
        if has_s:
            kb.dma([(o_sconv[:, :], oconv_s.t[:, :, :, :].rearrange("p a b c -> p (a b c)"))], R=[oconv_s.b])
            kb.dma([(o_ssc[:, :], osc_s.t[:, :, :, :].rearrange("p a b c -> p (a b c)"))], R=[osc_s.b])
        kb.barrier()
        MX.close()
        if full:
            if STAGE >= 2:
                attention_seg(si == 2, has_s, ntok)
            if STAGE >= 3:
                peer_seg(si == 2, has_s, ntok)
            S4 = ExitStack()
            rmsnormT(XT, ntok, PV_GFIN, None, S4, out_f32_dram=yT[:, c0:c0 + ntok])
            kb.barrier()
            S4.close()
    kb.dma([(o_pssm[:, :], STall.t[:, :, :].rearrange("p g f -> p (g f)"))], R=[STall.b])
    kb.dma([(o_pconv[:, :], halo_x.t[:, :, :].rearrange("p a b -> p (a b)"))], R=[halo_x.b])
    kb.dma([(o_psc[:, :], halo_sc.t[:, :, :].rearrange("p a b -> p (a b)"))], R=[halo_sc.b])
    kb.finish()
    A.close()
    G.close()
    return nc


def tile_w(W):
    K, N = W.shape
    return np.ascontiguousarray(W.reshape(K // 128, 128, N // 128, 128).transpose(2, 1, 0, 3))


_CACHE = {}


def make_cst():
    c = np.zeros((128, NCST), np.float32)
    s = np.arange(128)[:, None]
    l = np.arange(128)[None, :]
    c[:, C_ID:C_ID + 128] = (s == l)
    c[:, C_TRP:C_TRP + 128] = (s <= l)
    c[:, C_NGP:C_NGP + 128] = np.where(l >= s, 0.0, -30000.0)
    same = (s // 8) == (l // 8)
    c[:, C_TRS:C_TRS + 128] = (s <= l) & same
    c[:, C_NGS:C_NGS + 128] = np.where((l >= s) & same, 0.0, -30000.0)
    c[:, C_TOTS:C_TOTS + 128] = same
    c[:, C_RM:C_RM + 16] = (s // 8) == np.arange(16)[None, :]
    sel = (np.arange(128)[None, :] // 8) == np.arange(16)[:, None]
    c[:, C_SEL:C_SEL + 2048] = sel.reshape(1, 2048).astype(np.float32)
    c[:, C_IOTA:C_IOTA + 16] = np.arange(16)[None, :]
    c[:, C_IO128:C_IO128 + 128] = np.arange(128)[None, :]
    return c


def kernel(**inp):
    f = lambda k: np.asarray(inp[k], dtype=np.float32)
    x_prompt, x_sample = f("x_prompt"), f("x_sample")
    if "nc" not in _CACHE:
        _CACHE["nc"] = build()
    nc = _CACHE["nc"]
    w_in = f("w_in")[0]
    shared = {}
    shared["wz"] = tile_w(w_in[:, 0:2048])
    shared["wxbc"] = tile_w(w_in[:, 2048:5120])
    shared["wdt"] = np.ascontiguousarray(w_in[:, 5120:5152].reshape(16, 128, 32).transpose(1, 0, 2))
    shared["wsch"] = tile_w(w_in[:, 5152:7200])
    shared["wscb"] = tile_w(w_in[:, 7200:9248])
    shared["wscc"] = tile_w(w_in[:, 9248:11296])
    wo = f("w_out")[0]
    shared["wout"] = np.ascontiguousarray(wo.reshape(8, 4, 128, 16, 128).transpose(0, 3, 2, 1, 4))
    pv = np.zeros((128, NPV), np.float32)
    col = lambda v: v.reshape(-1, 128).T
    pv[:, PV_GMIX:PV_GMIX + 16] = col(f("norm_mix")[0])
    pv[:, PV_GMQ:PV_GMQ + 16] = col(f("norm_mem_q")[0])
    pv[:, PV_GMKV:PV_GMKV + 16] = col(f("norm_mem_kv")[0])
    pv[:, PV_GFFN:PV_GFFN + 16] = col(f("norm_ffn")[0])
    pv[:, PV_GFIN:PV_GFIN + 16] = col(f("norm_final"))
    pv[:, PV_SSDN:PV_SSDN + 16] = col(f("ssd_norm")[0])
    cw = f("ssd_conv_w")[0]
    pv[:, PV_CW:PV_CW + 96] = cw.reshape(4, 24, 128).transpose(2, 1, 0).reshape(128, 96)
    pv[:, PV_CB:PV_CB + 24] = col(f("ssd_conv_b")[0])
    scw = f("sc_conv_w")[0]
    pv[:, PV_SCW:PV_SCW + 48] = scw.reshape(3, 16, 128).transpose(2, 1, 0).reshape(128, 48)
    shared["pvec"] = pv
    tk = np.zeros((128, 96), np.float32)
    tk[:, 0:32] = f("ssd_dt_bias")[0][None, :]
    tk[:, 32:64] = f("ssd_a_log")[0][None, :]
    tk[:, 64:96] = f("ssd_d")[0][None, :]
    shared["tokp"] = tk
    shared["cst"] = make_cst()
    if STAGE >= 2:
        for nm, key in (("wmq", "w_mem_q"), ("wmk", "w_mem_k"), ("wmv", "w_mem_v"), ("wmo", "w_mem_o")):
            shared[nm] = tile_w(f(key)[0])
    if STAGE >= 3:
        shared["wpq"] = tile_w(f("w_peer_q")[0])
        sk = f("peer_sub_keys")[0]
        shared["skT"] = np.ascontiguousarray(sk.reshape(16, 128, 128).transpose(0, 2, 1))
        shared["pu"] = f("peer_u")[0]
        shared["pv"] = f("peer_v")[0]
    st_ssm, st_conv, st_sc = f("state_ssm")[0], f("state_ssd_conv")[0], f("state_short_conv")[0]
    in_maps = []
    for c in range(8):
        b, half = c // 2, c % 2
        sq = slice(16 * c, 16 * c + 16)
        own = x_prompt[b, half * 1024:(half + 1) * 1024]
        xs = x_sample[sq].reshape(128, 2048)
        m = dict(shared)
        m["xT"] = np.ascontiguousarray(np.concatenate([own, xs], 0).T)
        m["xpT"] = np.ascontiguousarray(x_prompt[b, 0:1024].T) if half == 1 else np.zeros((2048, 1024), np.float32)
        m["flag"] = np.full((128, 1), float(half), np.float32)
        m["stT"] = np.ascontiguousarray(st_ssm[sq].transpose(0, 3, 1, 2).reshape(16, 128, 2048))
        m["stconvT"] = np.ascontiguousarray(st_conv[sq].transpose(2, 0, 1).reshape(24, 128, 48))
        m["stscT"] = np.ascontiguousarray(st_sc[sq].transpose(2, 0, 1).reshape(16, 128, 32))
        if STAGE >= 2:
            m["memT"] = np.ascontiguousarray(f("mem_prompt")[b].T)
            m["ckT"] = np.ascontiguousarray(f("cache_mem_k")[0][sq].reshape(16, 256, 2048).transpose(0, 2, 1))
            m["cv"] = np.ascontiguousarray(f("cache_mem_v")[0][sq].reshape(16, 256, 2048))
        in_maps.append(m)
    res = run_bass_kernel_spmd(nc, in_maps, core_ids=list(range(8))).results
    y_p = np.zeros((4, 2048, 2048), np.float32)
    y_s = np.zeros((128, 8, 2048), np.float32)
    p_ssm = np.zeros((1, 4, 32, 64, 128), np.float32)
    p_conv = np.zeros((1, 4, 3, 3072), np.float32)
    p_sc = np.zeros((1, 4, 2, 2048), np.float32)
    p_mk = np.zeros((1, 4, 256, 4, 512), np.float32)
    p_mv = np.zeros((1, 4, 256, 4, 512), np.float32)
    s_ssm = np.zeros((1, 128, 32, 64, 128), np.float32)
    s_conv = np.zeros((1, 128, 3, 3072), np.float32)
    s_sc = np.zeros((1, 128, 2, 2048), np.float32)
    for c in range(8):
        r = res[c]
        b, half = c // 2, c % 2
        sq = slice(16 * c, 16 * c + 16)
        y = r["yT"].T
        y_p[b, half * 1024:(half + 1) * 1024] = y[0:1024]
        y_s[sq] = y[1024:1152].reshape(16, 8, 2048)
        s_ssm[0, sq] = r["o_sssm"].reshape(16, 128, 32, 64).transpose(0, 2, 3, 1)
        s_conv[0, sq] = r["o_sconv"].reshape(128, 24, 16, 3).transpose(2, 3, 1, 0).reshape(16, 3, 3072)
        s_sc[0, sq] = r["o_ssc"].reshape(128, 16, 16, 2).transpose(2, 3, 1, 0).reshape(16, 2, 2048)
        if half == 1:
            p_ssm[0, b] = r["o_pssm"].reshape(128, 32, 64).transpose(1, 2, 0)
            p_conv[0, b] = r["o_pconv"].reshape(128, 24, 3).transpose(2, 1, 0).reshape(3, 3072)
            p_sc[0, b] = r["o_psc"].reshape(128, 16, 2).transpose(2, 1, 0).reshape(2, 2048)
        if STAGE >= 2 and half == 0:
            p_mk[0, b] = r["o_mk"].T.reshape(256, 4, 512)
            p_mv[0, b] = r["o_mv"].T.reshape(256, 4, 512)
    return (y_p, y_s, p_ssm, p_conv, p_sc, p_mk, p_mv, s_ssm, s_conv, s_sc)
```
